# Optimizing a Trainium2 kernel written in Bass

```python
import jax, jax.numpy as jnp
from jax import lax
import numpy as np

D_MODEL = 1024
BATCH = 1
SEQ = 16384
DEPTH = 2

POOL_WINDOWS = (2, 4, 8, 16)
POOL_GROUP = 64
POOL_WIDTH = POOL_GROUP * len(POOL_WINDOWS)
ATTN_HEADS = 4
ATTN_HEAD_DIM = 64
ATTN_WIDTH = ATTN_HEADS * ATTN_HEAD_DIM
IDX_HEADS = 8
IDX_DIM = 64
TOPK_MAX = 256
Q_BLOCK = 128
HGRN_HEADS = 4
HGRN_EXPAND = 128
HGRN_HEAD_V = 128
HGRN_KW = HGRN_HEADS * HGRN_EXPAND
HGRN_VW = HGRN_HEADS * HGRN_HEAD_V
HGRN_CHUNK = 64
KEY_MAX = 1.0 - 1e-6
REL_BUCKETS = 32
REL_MAX_DIST = 128
D_FF = 2816
N_SUB = 3
DN_ALPHA = (2 * DEPTH) ** 0.25
DN_BETA = (8 * DEPTH) ** -0.25
LN_EPS = 1e-5
RMS_EPS = 1e-6

MIX_WIDTH = POOL_WIDTH + ATTN_WIDTH + HGRN_VW
MIX_IN_WIDTHS = (POOL_WIDTH, ATTN_WIDTH, ATTN_WIDTH, ATTN_WIDTH, IDX_HEADS * IDX_DIM, IDX_DIM, IDX_HEADS,
                 HGRN_KW, HGRN_KW, HGRN_VW, HGRN_VW, 3 * D_MODEL)
MIX_IN_WIDTH = sum(MIX_IN_WIDTHS)
MIX_SPLITS = tuple(int(s) for s in np.cumsum(MIX_IN_WIDTHS)[:-1])

kernel_name = 'hybrid_pool_dsa_hgrn2_block'


def layer_norm(x, g, b):
    x32 = x.astype(jnp.float32)
    mu = jnp.mean(x32, axis=-1, keepdims=True)
    var = jnp.mean(jnp.square(x32 - mu), axis=-1, keepdims=True)
    y = (x32 - mu) * lax.rsqrt(var + LN_EPS)
    return (y * g.astype(jnp.float32) + b.astype(jnp.float32)).astype(x.dtype)


def swiglu(h, w_in, w_out):
    gate, up = jnp.split(h @ w_in, 2, axis=-1)
    return (jax.nn.silu(gate) * up) @ w_out


def modulate(h, mod, j):
    return h * (1 + mod[:, j, 1][:, None, :]) + mod[:, j, 0][:, None, :]


def sub_gate(mod, j):
    return 1 + mod[:, j, 2][:, None, :]


def pool_mixer(a, pool_w, pool_scale):
    B, S, _ = a.shape
    a32 = a.astype(jnp.float32)
    P = jnp.pad(jnp.cumsum(a32, axis=1), ((0, 0), (1, 0), (0, 0)))
    pos1 = jnp.arange(1, S + 1)
    means = []
    for g, w in enumerate(POOL_WINDOWS):
        Pg = P[..., g * POOL_GROUP:(g + 1) * POOL_GROUP]
        lead = Pg[:, 1:]
        lag = jnp.concatenate([jnp.zeros_like(Pg[:, :w - 1]), Pg[:, :S + 1 - w]], axis=1)
        cnt = jnp.minimum(pos1, w).astype(jnp.float32)
        means.append((lead - lag) / cnt[None, :, None])
    pooled = jnp.stack(means, axis=2)
    d = (pooled - a32.reshape(B, S, len(POOL_WINDOWS), POOL_GROUP)).astype(a.dtype)
    y = jnp.einsum('bsgc,gcd->bsgd', d, pool_w)
    return y.reshape(B, S, POOL_WIDTH) * pool_scale


def t5_bucket(dist):
    max_exact = REL_BUCKETS // 2
    d32 = jnp.maximum(dist, 1).astype(jnp.float32)
    large = max_exact + (jnp.log(d32 / max_exact) / np.log(REL_MAX_DIST / max_exact)
                         * (REL_BUCKETS - max_exact)).astype(jnp.int32)
    large = jnp.minimum(large, REL_BUCKETS - 1)
    return jnp.where(dist < max_exact, dist, large)


def sparse_attention(q, k, v, q_idx, k_idx, w_idx, rel_bias):
    B, S, H, dh = q.shape
    top_k = min(TOPK_MAX, S // 4)
    n_blocks = S // Q_BLOCK
    scale = dh ** -0.5
    idx_scale = (IDX_DIM ** -0.5) * (IDX_HEADS ** -0.5)
    key_pos = jnp.arange(S)

    def block(i):
        q0 = i * Q_BLOCK
        qb = lax.dynamic_slice_in_dim(q, q0, Q_BLOCK, axis=1)
        qib = lax.dynamic_slice_in_dim(q_idx, q0, Q_BLOCK, axis=1)
        wb = lax.dynamic_slice_in_dim(w_idx, q0, Q_BLOCK, axis=1)
        qpos = q0 + jnp.arange(Q_BLOCK)
        logit_idx = jnp.einsum('bqhd,bsd->bqhs', qib, k_idx)
        score = jnp.einsum('bqh,bqhs->bqs', wb, jax.nn.relu(logit_idx)).astype(jnp.float32) * idx_scale
        causal = key_pos[None, :] <= qpos[:, None]
        score = jnp.where(causal[None], score, -jnp.inf)
        _, sel = lax.top_k(score, top_k)
        k_sel = jax.vmap(lambda kk, ii: kk[ii])(k, sel)
        v_sel = jax.vmap(lambda vv, ii: vv[ii])(v, sel)
        logits = jnp.einsum('bqhd,bqkhd->bhqk', qb, k_sel).astype(jnp.float32) * scale
        dist = qpos[None, :, None] - sel
        bias = rel_bias[t5_bucket(jnp.maximum(dist, 0))].astype(jnp.float32)
        logits = logits + jnp.transpose(bias, (0, 3, 1, 2))
        logits = jnp.where((dist >= 0)[:, None], logits, -jnp.inf)
        p = jax.nn.softmax(logits, axis=-1).astype(v.dtype)
        return jnp.einsum('bhqk,bqkhd->bqhd', p, v_sel)

    out = lax.map(block, jnp.arange(n_blocks))
    return jnp.transpose(out, (1, 0, 2, 3, 4)).reshape(B, S, H * dh)


def hgrn2(q, f_logit, i_val, g_out, lb, norm_g):
    B, S, _ = q.shape
    C = HGRN_CHUNK
    NC = S // C
    z = f_logit.astype(jnp.float32)
    lb = lb.astype(jnp.float32)
    key = (1 - lb) * jax.nn.sigmoid(-z)
    log_f = jnp.log1p(-jnp.minimum(key, KEY_MAX))
    qf = jax.nn.silu(q.astype(jnp.float32))

    def to_chunks(t, d):
        return jnp.transpose(t.reshape(B, NC, C, HGRN_HEADS, d), (1, 0, 3, 2, 4))

    qc = to_chunks(qf, HGRN_EXPAND)
    kc = to_chunks(key, HGRN_EXPAND)
    gc = to_chunks(log_f, HGRN_EXPAND)
    vc = to_chunks(i_val.astype(jnp.float32), HGRN_HEAD_V)
    tri = jnp.tril(jnp.ones((C, C), dtype=bool))

    def step(state, inp):
        qq, kk, vv, gg = inp
        A = jnp.cumsum(gg, axis=2)
        o_inter = jnp.einsum('bhtd,bhdv->bhtv', qq * jnp.exp(A), state)
        diff = A[:, :, :, None, :] - A[:, :, None, :, :]
        decay = jnp.exp(jnp.where(tri[None, None, :, :, None], diff, -jnp.inf))
        scores = jnp.einsum('bhtd,bhsd,bhtsd->bhts', qq, kk, decay)
        o_intra = jnp.einsum('bhts,bhsv->bhtv', scores, vv)
        k_dec = kk * jnp.exp(A[:, :, -1:, :] - A)
        new_state = jnp.exp(A[:, :, -1, :])[..., None] * state + jnp.einsum('bhsd,bhsv->bhdv', k_dec, vv)
        return new_state, o_inter + o_intra

    s0 = jnp.zeros((B, HGRN_HEADS, HGRN_EXPAND, HGRN_HEAD_V), jnp.float32)
    _, o = lax.scan(step, s0, (qc, kc, vc, gc))
    o = jnp.transpose(o, (1, 0, 3, 2, 4)).reshape(B, S, HGRN_HEADS, HGRN_HEAD_V)
    o = o * lax.rsqrt(jnp.mean(jnp.square(o), axis=-1, keepdims=True) + RMS_EPS)
    o = o.reshape(B, S, HGRN_VW).astype(q.dtype)
    return o * norm_g * jax.nn.silu(g_out)


def token_mixer(h, w_in, pool_w, pool_scale, rel_bias, lb, norm_g, w_branch, w_out):
    B, S, _ = h.shape
    proj = h @ w_in
    (a, q, k, v, qi, ki, wi, hq, hf, hi, hg, gates) = jnp.split(proj, MIX_SPLITS, axis=-1)
    y_a = pool_mixer(a, pool_w, pool_scale)
    y_b = sparse_attention(q.reshape(B, S, ATTN_HEADS, ATTN_HEAD_DIM),
                           k.reshape(B, S, ATTN_HEADS, ATTN_HEAD_DIM),
                           v.reshape(B, S, ATTN_HEADS, ATTN_HEAD_DIM),
                           qi.reshape(B, S, IDX_HEADS, IDX_DIM), ki, wi, rel_bias)
    y_c = hgrn2(hq, hf, hi, hg, lb, norm_g)
    g = jax.nn.sigmoid(gates.reshape(B, S, 3, D_MODEL))
    merged = (g[:, :, 0] * (y_a @ w_branch[:POOL_WIDTH])
              + g[:, :, 1] * (y_b @ w_branch[POOL_WIDTH:POOL_WIDTH + ATTN_WIDTH])
              + g[:, :, 2] * (y_c @ w_branch[POOL_WIDTH + ATTN_WIDTH:]))
    return merged @ w_out


def setup_inputs(seed: int = 0) -> dict:
    key = jax.random.key(seed)
    ks = jax.random.split(key, 16)
    f32 = jnp.float32

    def n(k, shape):
        return jax.random.normal(k, shape, f32)

    x = n(ks[0], (BATCH, SEQ, D_MODEL))
    c = n(ks[1], (BATCH, D_MODEL))
    w_ada = n(ks[2], (DEPTH, D_MODEL, N_SUB * 3 * D_MODEL)) * (0.3 * D_MODEL ** -0.5)
    b_ada = 0.01 * n(ks[3], (DEPTH, N_SUB * 3 * D_MODEL))
    ln_g = 1.0 + 0.02 * n(ks[4], (DEPTH, N_SUB, D_MODEL))
    ln_b = 0.02 * n(ks[5], (DEPTH, N_SUB, D_MODEL))
    ffn_w_in = n(ks[6], (DEPTH, 2, D_MODEL, 2 * D_FF)) * D_MODEL ** -0.5
    ffn_w_out = n(ks[7], (DEPTH, 2, D_FF, D_MODEL)) * (DN_BETA * D_FF ** -0.5)
    mix_w_in = n(ks[8], (DEPTH, D_MODEL, MIX_IN_WIDTH)) * D_MODEL ** -0.5
    pool_w = n(ks[9], (DEPTH, len(POOL_WINDOWS), POOL_GROUP, POOL_GROUP)) * POOL_GROUP ** -0.5
    pool_scale = 1.0 + 0.02 * n(ks[10], (DEPTH, POOL_WIDTH))
    rel_bias = 0.5 * n(ks[11], (REL_BUCKETS, ATTN_HEADS))
    hgrn_lb = 1.0 + 0.1 * n(ks[12], (DEPTH, HGRN_KW))
    hgrn_norm_g = 1.0 + 0.02 * n(ks[13], (DEPTH, HGRN_VW))
    row_scale = jnp.concatenate([jnp.full((POOL_WIDTH,), POOL_WIDTH ** -0.5, f32),
                                 jnp.full((ATTN_WIDTH,), ATTN_WIDTH ** -0.5, f32),
                                 jnp.full((HGRN_VW,), HGRN_VW ** -0.5, f32)])
    w_branch = n(ks[14], (DEPTH, MIX_WIDTH, D_MODEL)) * row_scale[None, :, None]
    w_out = n(ks[15], (DEPTH, D_MODEL, D_MODEL)) * (DN_BETA * D_MODEL ** -0.5)
    return {'x': x, 'c': c, 'w_ada': w_ada, 'b_ada': b_ada, 'ln_g': ln_g, 'ln_b': ln_b,
            'ffn_w_in': ffn_w_in, 'ffn_w_out': ffn_w_out, 'mix_w_in': mix_w_in,
            'pool_w': pool_w, 'pool_scale': pool_scale, 'rel_bias': rel_bias,
            'hgrn_lb': hgrn_lb, 'hgrn_norm_g': hgrn_norm_g, 'w_branch': w_branch, 'w_out': w_out}


def reference(x, c, w_ada, b_ada, ln_g, ln_b, ffn_w_in, ffn_w_out, mix_w_in, pool_w, pool_scale,
              rel_bias, hgrn_lb, hgrn_norm_g, w_branch, w_out):
    B = x.shape[0]
    lb_sm = jax.nn.softmax(hgrn_lb.astype(jnp.float32), axis=0)
    lbs = jnp.cumsum(lb_sm, axis=0) - lb_sm[0:1]
    cond = jax.nn.silu(c)
    for l in range(DEPTH):
        mod = (cond @ w_ada[l] + b_ada[l]).reshape(B, N_SUB, 3, D_MODEL)
        y = swiglu(modulate(x, mod, 0), ffn_w_in[l, 0], ffn_w_out[l, 0])
        x = layer_norm(DN_ALPHA * x + 0.5 * sub_gate(mod, 0) * y, ln_g[l, 0], ln_b[l, 0])
        y = token_mixer(modulate(x, mod, 1), mix_w_in[l], pool_w[l], pool_scale[l], rel_bias,
                        lbs[l], hgrn_norm_g[l], w_branch[l], w_out[l])
        x = layer_norm(DN_ALPHA * x + sub_gate(mod, 1) * y, ln_g[l, 1], ln_b[l, 1])
        y = swiglu(modulate(x, mod, 2), ffn_w_in[l, 1], ffn_w_out[l, 1])
        x = layer_norm(DN_ALPHA * x + 0.5 * sub_gate(mod, 2) * y, ln_g[l, 2], ln_b[l, 2])
    return x
```

```python
import numpy as np
from contextlib import ExitStack
import ml_dtypes
import concourse.bass as bass
import concourse.mybir as mybir
from concourse.bass_utils import run_bass_kernel_spmd

F32 = mybir.dt.float32
BF16 = mybir.dt.bfloat16
AF = mybir.ActivationFunctionType
ALU = mybir.AluOpType
AX = mybir.AxisListType
NPBF = ml_dtypes.bfloat16

NCORE = 8
D = 1024
SEQ = 16384
TPC = SEQ // NCORE
DFF = 2816
NFC = DFF // 128
DEPTH = 2
ALPHA = float((2 * DEPTH) ** 0.25)
LN_EPS = 1e-5
RMS_EPS = 1e-6
KEY_MAX = 1.0 - 1e-6
C_A, C_Q, C_K, C_V, C_QI, C_KI, C_WI, C_HQ, C_HF, C_HI, C_HG, C_G = 0, 256, 512, 768, 1024, 1536, 1600, 1608, 2120, 2632, 3144, 3656
TOPK = 256


class Buf:
    def __init__(self, t, nsub=1, name=""):
        self.t = t
        self.nsub = nsub
        self.name = name
        self.w = [None] * nsub
        self.r = [dict() for _ in range(nsub)]
        self.dsem = [None] * nsub
        self.dcnt = [0] * nsub

    def __getitem__(self, idx):
        return self.t[idx]


def _regions(spec):
    out = []
    for s in spec:
        if isinstance(s, Buf):
            out.extend((s, i) for i in range(s.nsub))
        else:
            b, idx = s
            if idx is None:
                out.extend((b, i) for i in range(b.nsub))
            elif isinstance(idx, int):
                out.append((b, idx))
            else:
                out.extend((b, i) for i in idx)
    return out


class Lazy:
    def __init__(self, fn):
        self.fn = fn


class _RecIns:
    def __init__(self, entry):
        self.entry = entry

    def then_inc(self, sem, n=None):
        self.entry["inc"] = (sem, n)
        return self


class _RecEng:
    def __init__(self, q):
        self.q = q

    def __getattr__(self, name):
        def f(*a, **k):
            entry = {"name": name, "a": a, "k": k, "inc": None}
            self.q.append(entry)
            return _RecIns(entry)
        return f


class Sched:
    ENG = ("pe", "act", "dve", "pool", "sp")

    def __init__(self, nc, ctx):
        self.nc = nc
        self.ctx = ctx
        self.q = {e: [] for e in self.ENG}
        self.eng = {e: _RecEng(self.q[e]) for e in self.ENG}
        self.sems = {}
        self.cnt = {}
        self.waited = {e: {} for e in self.eng}
        for e in self.eng:
            self.sems[e] = ctx.enter_context(nc.semaphore("s_" + e))
            self.cnt[e] = 0
        self.nsem = len(self.eng)
        self.pbanks = []
        self.pidx = 0
        self.ccn = 0
        self.pre = {e: [] for e in self.ENG}

    def sbuf(self, name, shape, dt, nsub=1):
        t = self.ctx.enter_context(self.nc.sbuf_tensor(name, list(shape), dt))
        return Buf(t, nsub, name)

    def psum(self, name, shape, dt, nsub=1):
        t = self.ctx.enter_context(self.nc.psum_tensor(name, list(shape), dt))
        return Buf(t, nsub, name)

    def dram(self, name, shape, dt, kind=None, nsub=1):
        if kind is None:
            t = self.nc.dram_tensor(name, list(shape), dt).ap()
        else:
            t = self.nc.dram_tensor(name, list(shape), dt, kind=kind).ap()
        return Buf(t, nsub, name)

    def mkbanks(self, n):
        self.pbanks = [self.psum("pb%d" % i, [128, 512], F32) for i in range(n)]

    def bank(self):
        b = self.pbanks[self.pidx % len(self.pbanks)]
        self.pidx += 1
        return b

    def _need(self, e, tok, needs):
        if tok is None:
            return
        k, v = tok
        if k == e and e == "pe":
            return
        if self.waited[e].get(k, 0) >= v:
            return
        if needs.get(k, 0) < v:
            needs[k] = v

    def _deps(self, e, R, W):
        needs = {}
        for b, i in R:
            self._need(e, b.w[i], needs)
        for b, i in W:
            self._need(e, b.w[i], needs)
            for k, v in b.r[i].items():
                self._need(e, (k, v), needs)
        for k, v in needs.items():
            self.eng[e].wait_ge(self.sems[k], v)
            self.waited[e][k] = v

    def op(self, e, fn, R=(), W=()):
        R = _regions(R)
        W = _regions(W)
        self._deps(e, R, W)
        ins = fn(self.eng[e])
        self.cnt[e] += 1
        ins.then_inc(self.sems[e], 1)
        tok = (e, self.cnt[e])
        for b, i in R:
            b.r[i][e] = self.cnt[e]
        for b, i in W:
            b.w[i] = tok
            b.r[i] = {}
        return tok

    def dma(self, e, out, in_, R=(), W=(), **kw):
        R = _regions(R)
        W = _regions(W)
        self._deps(e, R, W)
        b0, i0 = W[0]
        if b0.dsem[i0] is None:
            key = "d%d" % self.nsem
            self.nsem += 1
            self.sems[key] = self.ctx.enter_context(self.nc.semaphore(key))
            b0.dsem[i0] = key
        key = b0.dsem[i0]
        b0.dcnt[i0] += 16
        ins = self.eng[e].dma_start(out=out, in_=in_, **kw)
        ins.then_inc(self.sems[key], 16)
        tok = (key, b0.dcnt[i0])
        for b, i in R:
            b.r[i][key] = b0.dcnt[i0]
        for b, i in W:
            b.w[i] = tok
            b.r[i] = {}
        return tok

    def allgather(self, src, dst):
        R = _regions([src])
        W = _regions([dst])
        self._deps("pool", R, W)
        if "cc" not in self.sems:
            self.sems["cc"] = self.ctx.enter_context(self.nc.semaphore("s_cc"))
        self.ccn += 1
        ins = self.eng["pool"].collective_compute("AllGather", ALU.bypass, replica_groups=[list(range(NCORE))],
                                                  ins=[src.t.opt()], outs=[dst.t.opt()])
        ins.then_inc(self.sems["cc"])
        self.eng["pool"].wait_ge(self.sems["cc"], self.ccn)
        return self.op("pool", lambda e: e.nop(), R=[src], W=[dst])

    def finish(self, outs, e="sp"):
        needs = {}
        for b in outs:
            for i in range(b.nsub):
                self._need(e, b.w[i], needs)
        for k, v in needs.items():
            self.eng[e].wait_ge(self.sems[k], v)
        self.emit()

    def emit(self):
        nc = self.nc

        def replay(engine, q):
            for en in q:
                a = [x.fn() if isinstance(x, Lazy) else x for x in en["a"]]
                k = {kk: (x.fn() if isinstance(x, Lazy) else x) for kk, x in en["k"].items()}
                ins = getattr(engine, en["name"])(*a, **k)
                if en["inc"] is not None:
                    sem, n = en["inc"]
                    if n is None:
                        ins.then_inc(sem)
                    else:
                        ins.then_inc(sem, n)

        with nc.Block() as block:
            @block.sync
            def _(eng):
                for f in self.pre["sp"]:
                    f(eng)
                replay(eng, self.q["sp"])

            @block.gpsimd
            def _(eng):
                for f in self.pre["pool"]:
                    f(eng)
                replay(eng, self.q["pool"])

            @block.scalar
            def _(eng):
                replay(eng, self.q["act"])

            @block.vector
            def _(eng):
                replay(eng, self.q["dve"])

            @block.tensor
            def _(eng):
                replay(eng, self.q["pe"])


def _fm(ap):
    return ap.rearrange("(c p) t -> p c t", p=128)


def emit_ln(S, xs, sub_of, col0, lng, lnb, onesm, tmp):
    sl = slice(col0, col0 + 512)
    pm = S.bank()
    pq = S.bank()
    for d in range(8):
        S.op("pe", lambda e: e.matmul(pm[:, :], onesm[:, :], xs[:, d, sl], start=(d == 0), stop=(d == 7)),
             R=[(xs, sub_of(d)), onesm], W=[pm])
    for d in range(8):
        sq = tmp["sq"][d % 2]
        S.op("act", lambda e: e.activation(sq[:, :], xs[:, d, sl], AF.Square), R=[(xs, sub_of(d))], W=[sq])
        S.op("pe", lambda e: e.matmul(pq[:, :], onesm[:, :], sq[:, :], start=(d == 0), stop=(d == 7)),
             R=[sq, onesm], W=[pq])
    mean = tmp["mean"]
    rstd = tmp["rstd"]
    m2 = tmp["m2"]
    S.op("act", lambda e: e.copy(mean[:, :], pm[:, :]), R=[pm], W=[mean])
    S.op("dve", lambda e: e.tensor_tensor(m2[:, :], mean[:, :], mean[:, :], ALU.mult), R=[mean], W=[m2])
    S.op("dve", lambda e: e.tensor_tensor(m2[:, :], pq[:, :], m2[:, :], ALU.subtract), R=[pq, m2], W=[m2])
    S.op("dve", lambda e: e.tensor_scalar(m2[:, :], m2[:, :], 0.0, LN_EPS, ALU.max, ALU.add), R=[m2], W=[m2])
    S.op("act", lambda e: e.activation(m2[:, :], m2[:, :], AF.Sqrt), R=[m2], W=[m2])
    S.op("dve", lambda e: e.reciprocal(rstd[:, :], m2[:, :]), R=[m2], W=[rstd])
    for d in range(8):
        t = tmp["t"][d % 2]
        S.op("pool", lambda e: e.tensor_tensor(t[:, :], xs[:, d, sl], mean[:, :], ALU.subtract),
             R=[(xs, sub_of(d)), mean], W=[t])
        S.op("dve", lambda e: e.tensor_tensor(t[:, :], t[:, :], rstd[:, :], ALU.mult), R=[t, rstd], W=[t])
        S.op("act", lambda e: e.activation(xs[:, d, sl], t[:, :], AF.Identity, bias=lnb[:, d:d + 1], scale=lng[:, d:d + 1]),
             R=[t, lng, lnb], W=[(xs, sub_of(d))])


def ln_tmps(S):
    return {"sq": [S.sbuf("ln_sq%d" % i, [128, 512], F32) for i in range(2)],
            "t": [S.sbuf("ln_t%d" % i, [128, 512], F32) for i in range(2)],
            "mean": S.sbuf("ln_mean", [128, 512], F32), "rstd": S.sbuf("ln_rstd", [128, 512], F32),
            "m2": S.sbuf("ln_m2", [128, 512], F32)}


def build_mod():
    nc = bass.Bass("TRN2", target_bir_lowering=False)
    with ExitStack() as ctx:
        S = Sched(nc, ctx)
        cT = S.dram("cT", [128, 8], F32, "ExternalInput")
        wsl = S.dram("wsl", [18, 1024, 128], F32, "ExternalInput")
        bsl = S.dram("bsl", [128, 18], F32, "ExternalInput")
        modo = S.dram("modo", [128, 18], F32, "ExternalOutput")
        c_sb = S.sbuf("c_sb", [128, 8], F32)
        cond = S.sbuf("cond", [128, 8, 2], F32)
        b_sb = S.sbuf("b_sb", [128, 18], F32)
        o_sb = S.sbuf("o_sb", [128, 18], F32)
        w_sb = [S.sbuf("w_sb%d" % i, [128, 8, 128], F32) for i in range(18)]
        ps = S.psum("ps", [128, 18, 2], F32)
        S.dma("sp", c_sb[:, :], cT[:, :], R=[cT], W=[c_sb])
        S.dma("sp", b_sb[:, :], bsl[:, :], R=[bsl], W=[b_sb])
        for i in range(18):
            S.dma("sp", w_sb[i][:, :, :], wsl[i].rearrange("(kc p) n -> p kc n", p=128), R=[wsl], W=[w_sb[i]])
        for r in range(2):
            S.op("act", lambda e: e.activation(cond[:, :, r], c_sb[:, :], AF.Silu), R=[c_sb], W=[cond])
        for i in range(18):
            for kc in range(8):
                S.op("pe", lambda e: e.matmul(ps[:, i, :], w_sb[i][:, kc, :], cond[:, kc, :], start=(kc == 0), stop=(kc == 7)),
                     R=[w_sb[i], cond], W=[ps])
        S.op("dve", lambda e: e.tensor_tensor(o_sb[:, :], ps[:, :, 0], b_sb[:, :], ALU.add), R=[ps, b_sb], W=[o_sb])
        S.dma("sp", modo[:, :], o_sb[:, :], R=[o_sb], W=[modo])
        S.finish([modo])
    return nc


def build_ffn():
    nc = bass.Bass("TRN2", target_bir_lowering=False)
    T = 1024
    with ExitStack() as ctx:
        S = Sched(nc, ctx)
        xT = S.dram("xT", [D, TPC], F32, "ExternalInput")
        modv = S.dram("modv", [128, 3, 8], F32, "ExternalInput")
        lngd = S.dram("lng", [128, 8], F32, "ExternalInput")
        lnbd = S.dram("lnb", [128, 8], F32, "ExternalInput")
        w_in = S.dram("w_in", [D, 2 * DFF], F32, "ExternalInput")
        w_out = S.dram("w_out", [DFF, D], F32, "ExternalInput")
        onesd = S.dram("onesm", [128, 128], F32, "ExternalInput")
        xo = S.dram("xo", [D, TPC], F32, "ExternalOutput", nsub=2)

        x_sb = S.sbuf("x_sb", [128, 8, T], F32, nsub=16)
        hb = S.sbuf("hb", [128, 8, T], BF16)
        act = S.sbuf("act", [128, NFC, T], BF16, nsub=NFC * 2)
        wi_sb = [S.sbuf("wi_sb%d" % i, [128, 8, 2, 256], BF16) for i in range(2)]
        wo_sb = [S.sbuf("wo_sb%d" % i, [128, NFC, 256], BF16) for i in range(4)]
        sg = [S.sbuf("sg%d" % i, [128, 512], F32) for i in range(2)]
        mod_sb = S.sbuf("mod_sb", [128, 3, 8], F32)
        sc1 = S.sbuf("sc1", [128, 8], F32)
        g2 = S.sbuf("g2", [128, 8], F32)
        lng = S.sbuf("lng_sb", [128, 8], F32)
        lnb = S.sbuf("lnb_sb", [128, 8], F32)
        onesm = S.sbuf("ones_sb", [128, 128], F32)
        tmp = ln_tmps(S)
        S.mkbanks(8)

        S.dma("sp", mod_sb[:, :, :], modv[:, :, :], R=[modv], W=[mod_sb])
        S.dma("sp", lng[:, :], lngd[:, :], R=[lngd], W=[lng])
        S.dma("sp", lnb[:, :], lnbd[:, :], R=[lnbd], W=[lnb])
        S.dma("sp", onesm[:, :], onesd[:, :], R=[onesd], W=[onesm])
        S.op("dve", lambda e: e.tensor_scalar(sc1[:, :], mod_sb[:, 1, :], 1.0, None, ALU.add), R=[mod_sb], W=[sc1])
        S.op("dve", lambda e: e.tensor_scalar(g2[:, :], mod_sb[:, 2, :], 1.0, 0.5, ALU.add, ALU.mult), R=[mod_sb], W=[g2])
        w_in_v = w_in.t.rearrange("(kc p) n -> p kc n", p=128)
        w_out_v = w_out.t.rearrange("(f p) n -> p f n", p=128)
        for q in range(4):
            S.dma("pool", wo_sb[q][:, :, :], w_out_v[:, :, q * 256:(q + 1) * 256], R=[w_out], W=[wo_sb[q]])
        xT_v = _fm(xT.t)
        xo_v = _fm(xo.t)
        for tt in range(2):
            tsl = slice(tt * T, (tt + 1) * T)
            for d in range(8):
                S.dma("sp", x_sb[:, d, :], xT_v[:, d, tsl], R=[xT], W=[(x_sb, [2 * d, 2 * d + 1])])
            for d in range(8):
                S.op("act", lambda e: e.activation(hb[:, d, :], x_sb[:, d, :], AF.Identity, bias=mod_sb[:, 0, d:d + 1], scale=sc1[:, d:d + 1]),
                     R=[(x_sb, [2 * d, 2 * d + 1]), mod_sb, sc1], W=[hb])
            for d in range(8):
                S.op("pool", lambda e: e.tensor_scalar(x_sb[:, d, :], x_sb[:, d, :], ALPHA, 0.0, ALU.mult, ALU.add),
                     R=[(x_sb, [2 * d, 2 * d + 1])], W=[(x_sb, [2 * d, 2 * d + 1])])
            for gi in range(NFC // 2):
                wb = wi_sb[gi % 2]
                S.dma("pool", wb[:, :, 0, :], w_in_v[:, :, gi * 256:(gi + 1) * 256], R=[w_in], W=[wb])
                S.dma("pool", wb[:, :, 1, :], w_in_v[:, :, DFF + gi * 256:DFF + (gi + 1) * 256], R=[w_in], W=[wb])
                for jj in range(2):
                    j = gi * 2 + jj
                    for hh in range(2):
                        hs = slice(hh * 512, (hh + 1) * 512)
                        pg = S.bank()
                        pu = S.bank()
                        for kc in range(8):
                            S.op("pe", lambda e: e.matmul(pg[:, :], wb[:, kc, 0, jj * 128:(jj + 1) * 128], hb[:, kc, hs], start=(kc == 0), stop=(kc == 7)),
                                 R=[wb, hb], W=[pg])
                        for kc in range(8):
                            S.op("pe", lambda e: e.matmul(pu[:, :], wb[:, kc, 1, jj * 128:(jj + 1) * 128], hb[:, kc, hs], start=(kc == 0), stop=(kc == 7)),
                                 R=[wb, hb], W=[pu])
                        s = sg[(j * 2 + hh) % 2]
                        S.op("act", lambda e: e.activation(s[:, :], pg[:, :], AF.Silu), R=[pg], W=[s])
                        S.op("dve", lambda e: e.tensor_tensor(act[:, j, hs], pu[:, :], s[:, :], ALU.mult), R=[pu, s], W=[(act, j * 2 + hh)])
            for hh in range(2):
                hs = slice(hh * 512, (hh + 1) * 512)
                for d in range(8):
                    po = S.bank()
                    for f in range(NFC):
                        S.op("pe", lambda e: e.matmul(po[:, :], wo_sb[d // 2][:, f, (d % 2) * 128:(d % 2 + 1) * 128], act[:, f, hs], start=(f == 0), stop=(f == NFC - 1)),
                             R=[wo_sb[d // 2], (act, f * 2 + hh)], W=[po])
                    S.op("dve", lambda e: e.scalar_tensor_tensor(x_sb[:, d, hs], po[:, :], g2[:, d:d + 1], x_sb[:, d, hs], ALU.mult, ALU.add),
                         R=[po, g2, (x_sb, 2 * d + hh)], W=[(x_sb, 2 * d + hh)])
                emit_ln(S, x_sb, lambda d: 2 * d + hh, hh * 512, lng, lnb, onesm, tmp)
            for d in range(8):
                S.dma("sp", xo_v[:, d, tsl], x_sb[:, d, :], R=[(x_sb, [2 * d, 2 * d + 1])], W=[(xo, tt)])
        S.finish([xo])
    return nc


def build_proj():
    nc = bass.Bass("TRN2", target_bir_lowering=False)
    T = 512
    with ExitStack() as ctx:
        S = Sched(nc, ctx)
        xT = S.dram("xT", [D, TPC], F32, "ExternalInput")
        modv = S.dram("modv", [128, 3, 8], F32, "ExternalInput")
        wmix = S.dram("wmix", [D, 6728], F32, "ExternalInput")
        qT = S.dram("qT", [256, TPC], BF16, "ExternalOutput")
        kT = S.dram("kT", [256, TPC], BF16, "ExternalOutput")
        qiT = S.dram("qiT", [512, TPC], BF16, "ExternalOutput")
        kiT = S.dram("kiT", [64, TPC], BF16, "ExternalOutput")
        hqT = S.dram("hqT", [512, TPC], F32, "ExternalOutput")
        hfT = S.dram("hfT", [512, TPC], F32, "ExternalOutput")
        vaug = S.dram("vaug", [TPC, 260], BF16, "ExternalOutput")
        hitm = S.dram("hitm", [TPC, 512], BF16, "ExternalOutput")
        witm = S.dram("witm", [TPC, 8], F32, "ExternalOutput")
        outs = [qT, kT, qiT, kiT, hqT, hfT, vaug, hitm, witm]

        mod_sb = S.sbuf("mod_sb", [128, 3, 8], F32)
        sc1 = S.sbuf("sc1", [128, 8], F32)
        wfm = S.sbuf("wfm", [128, 8, 2112], BF16)
        wtm = S.sbuf("wtm", [128, 8, 776], BF16)
        x_sb = [S.sbuf("x_sb%d" % i, [128, 8, T], F32) for i in range(2)]
        hb = [S.sbuf("hb%d" % i, [128, 8, T], BF16) for i in range(2)]
        st_b = [S.sbuf("st_b%d" % i, [128, 9, T], BF16) for i in range(2)]
        st_f = [S.sbuf("st_f%d" % i, [128, 8, T], F32) for i in range(2)]
        st_v = [S.sbuf("st_v%d" % i, [128, 4, 4, 65], BF16) for i in range(2)]
        st_h = [S.sbuf("st_h%d" % i, [128, 4, 512], BF16) for i in range(2)]
        st_w = [S.sbuf("st_w%d" % i, [128, 4, 8], F32) for i in range(2)]
        S.mkbanks(8)
        S.dma("sp", mod_sb[:, :, :], modv[:, :, :], R=[modv], W=[mod_sb])
        S.op("dve", lambda e: e.tensor_scalar(sc1[:, :], mod_sb[:, 1, :], 1.0, None, ALU.add), R=[mod_sb], W=[sc1])
        wv = wmix.t.rearrange("(kc p) n -> p kc n", p=128)
        for (l0, c0, n) in [(0, C_Q, 512), (512, C_QI, 576), (1088, C_HQ, 1024)]:
            S.dma("pool", wfm[:, :, l0:l0 + n], wv[:, :, c0:c0 + n], R=[wmix], W=[wfm])
        for (l0, c0, n) in [(0, C_V, 256), (256, C_WI, 8), (264, C_HI, 512)]:
            S.dma("pool", wtm[:, :, l0:l0 + n], wv[:, :, c0:c0 + n], R=[wmix], W=[wtm])
        for i in range(2):
            S.op("pool", lambda e: e.memset(st_v[i][:, :, :, 64:65], 1.0), W=[st_v[i]])
        xv = _fm(xT.t)
        fm_list = [(j * 128, 128, "b", j) for j in range(8)] + [(1024, 64, "b", 8)] + [(1088 + j * 128, 128, "f", j) for j in range(8)]
        ev = 0
        for tt in range(TPC // T):
            tsl = slice(tt * T, (tt + 1) * T)
            b = tt % 2
            S.dma("sp", x_sb[b][:, :, :], xv[:, :, tsl], R=[xT], W=[x_sb[b]])
            for d in range(8):
                S.op("act", lambda e: e.activation(hb[b][:, d, :], x_sb[b][:, d, :], AF.Identity, bias=mod_sb[:, 0, d:d + 1], scale=sc1[:, d:d + 1]),
                     R=[x_sb[b], mod_sb, sc1], W=[hb[b]])
            for (l0, M, kind, idx) in fm_list:
                p = S.bank()
                for kc in range(8):
                    S.op("pe", lambda e: e.matmul(p[0:M, :], wfm[:, kc, l0:l0 + M], hb[b][:, kc, :], start=(kc == 0), stop=(kc == 7)),
                         R=[wfm, hb[b]], W=[p])
                dst = st_b[b] if kind == "b" else st_f[b]
                if ev % 2 == 0:
                    S.op("act", lambda e: e.copy(dst[0:M, idx, :], p[0:M, :]), R=[p], W=[dst])
                else:
                    S.op("dve", lambda e: e.tensor_copy(dst[0:M, idx, :], p[0:M, :]), R=[p], W=[dst])
                ev += 1
            for sub in range(4):
                ss = slice(sub * 128, (sub + 1) * 128)
                pv = S.bank()
                ph = S.bank()
                for kc in range(8):
                    S.op("pe", lambda e: e.matmul(pv[:, 0:264], hb[b][:, kc, ss], wtm[:, kc, 0:264], start=(kc == 0), stop=(kc == 7)),
                         R=[wtm, hb[b]], W=[pv])
                for kc in range(8):
                    S.op("pe", lambda e: e.matmul(ph[:, :], hb[b][:, kc, ss], wtm[:, kc, 264:776], start=(kc == 0), stop=(kc == 7)),
                         R=[wtm, hb[b]], W=[ph])
                S.op("dve", lambda e: e.tensor_copy(st_v[b][:, sub, :, 0:64], pv[:, 0:256].rearrange("p (h e) -> p h e", h=4)), R=[pv], W=[st_v[b]])
                S.op("dve", lambda e: e.tensor_copy(st_w[b][:, sub, :], pv[:, 256:264]), R=[pv], W=[st_w[b]])
                S.op("act", lambda e: e.copy(st_h[b][:, sub, :], ph[:, :]), R=[ph], W=[st_h[b]])
            S.dma("sp", _fm(qT.t)[:, :, tsl], st_b[b][:, 0:2, :], R=[st_b[b]], W=[qT])
            S.dma("sp", _fm(kT.t)[:, :, tsl], st_b[b][:, 2:4, :], R=[st_b[b]], W=[kT])
            S.dma("sp", _fm(qiT.t)[:, :, tsl], st_b[b][:, 4:8, :], R=[st_b[b]], W=[qiT])
            S.dma("sp", kiT[:, tsl], st_b[b][0:64, 8, :], R=[st_b[b]], W=[kiT])
            S.dma("sp", _fm(hqT.t)[:, :, tsl], st_f[b][:, 0:4, :], R=[st_f[b]], W=[hqT])
            S.dma("sp", _fm(hfT.t)[:, :, tsl], st_f[b][:, 4:8, :], R=[st_f[b]], W=[hfT])
            S.dma("sp", vaug.t.rearrange("(i p) f -> p i f", p=128)[:, tt * 4:(tt + 1) * 4, :], st_v[b][:, :, :, :].rearrange("p i h e -> p i (h e)"), R=[st_v[b]], W=[vaug])
            S.dma("sp", hitm.t.rearrange("(i p) f -> p i f", p=128)[:, tt * 4:(tt + 1) * 4, :], st_h[b][:, :, :], R=[st_h[b]], W=[hitm])
            S.dma("sp", witm.t.rearrange("(i p) f -> p i f", p=128)[:, tt * 4:(tt + 1) * 4, :], st_w[b][:, :, :], R=[st_w[b]], W=[witm])
        S.finish(outs)
    return nc


def build_hg():
    nc = bass.Bass("TRN2", target_bir_lowering=False)
    TS = 2048
    C = 64
    NCH = TS // C
    with ExitStack() as ctx:
        S = Sched(nc, ctx)
        hqT = S.dram("hqT", [128, SEQ], F32, "ExternalInput")
        hfT = S.dram("hfT", [128, SEQ], F32, "ExternalInput")
        vtm = S.dram("vtm", [SEQ, 64], BF16, "ExternalInput")
        lbraw = S.dram("lbraw", [128, 2], F32, "ExternalInput")
        lcoef = S.dram("lcoef", [128, 2], F32, "ExternalInput")
        trid = S.dram("triT", [64, 64], F32, "ExternalInput")
        identd = S.dram("identb", [128, 128], BF16, "ExternalInput")
        rmaskd = S.dram("rmask", [128, TS], F32, "ExternalInput")
        oT = S.dram("oT", [64, SEQ], F32, "ExternalOutput")

        v_sb = S.sbuf("v_sb", [64, SEQ // C, 64], BF16)
        lb_sb = S.sbuf("lb_sb", [128, 2], F32)
        lc_sb = S.sbuf("lc_sb", [128, 2], F32)
        sm_sb = S.sbuf("sm_sb", [128, 2], F32)
        t1 = S.sbuf("t1", [128, 1], F32)
        oml = S.sbuf("oml", [128, 1], F32)
        tri = S.sbuf("tri", [64, 64], F32)
        ident = S.sbuf("ident", [128, 128], BF16)
        rmask = S.sbuf("rmask_sb", [128, TS], F32)
        hq_sb = [S.sbuf("hq_sb%d" % i, [128, TS], F32) for i in range(2)]
        hf_sb = [S.sbuf("hf_sb%d" % i, [128, TS], F32) for i in range(2)]
        keyf = S.sbuf("keyf", [128, TS], F32)
        lgf = S.sbuf("lgf", [128, TS], F32)
        A = S.sbuf("A", [128, TS], F32)
        eA = S.sbuf("eA", [128, TS], F32)
        eAn = S.sbuf("eAn", [128, TS], F32)
        qt = S.sbuf("qt", [128, TS], BF16)
        kt = S.sbuf("kt", [128, TS], BF16)
        ktT = S.sbuf("ktT", [64, NCH, 128], BF16, nsub=NCH // 4)
        o_sb = [S.sbuf("o_sb%d" % i, [64, TS], F32) for i in range(2)]
        Sst = S.sbuf("Sst", [128, 64], F32)
        Sb = S.sbuf("Sb", [128, 64], BF16)
        tS = [S.sbuf("tS%d" % i, [128, 64], F32) for i in range(2)]
        smb = [S.sbuf("smb%d" % i, [64, 64], BF16) for i in range(2)]
        ps_s = [S.psum("ps_s%d" % i, [128, 512], F32) for i in range(2)]
        ps_o = [S.psum("ps_o%d" % i, [128, 512], F32) for i in range(2)]
        ps_d = [S.psum("ps_d%d" % i, [128, 512], F32) for i in range(2)]
        ptr = [S.psum("ptr%d" % i, [128, 1024], BF16) for i in range(2)]

        S.dma("sp", lb_sb[:, :], lbraw[:, :], R=[lbraw], W=[lb_sb])
        S.dma("sp", lc_sb[:, :], lcoef[:, :], R=[lcoef], W=[lc_sb])
        S.dma("sp", tri[:, :], trid[:, :], R=[trid], W=[tri])
        S.dma("sp", ident[:, :], identd[:, :], R=[identd], W=[ident])
        S.dma("sp", rmask[:, :], rmaskd[:, :], R=[rmaskd], W=[rmask])
        vv = vtm.t.rearrange("(c p) f -> p c f", p=64)
        for i in range(4):
            S.dma("sp", v_sb[:, i * 64:(i + 1) * 64, :], vv[:, i * 64:(i + 1) * 64, :], R=[vtm], W=[v_sb])
        S.op("dve", lambda e: e.tensor_tensor(t1[:, :], lb_sb[:, 0:1], lb_sb[:, 1:2], ALU.max), R=[lb_sb], W=[t1])
        S.op("dve", lambda e: e.tensor_scalar(t1[:, :], t1[:, :], -1.0, None, ALU.mult), R=[t1], W=[t1])
        S.op("act", lambda e: e.activation(sm_sb[:, :], lb_sb[:, :], AF.Exp, bias=t1[:, 0:1], scale=1.0), R=[lb_sb, t1], W=[sm_sb])
        S.op("dve", lambda e: e.tensor_tensor(t1[:, :], sm_sb[:, 0:1], sm_sb[:, 1:2], ALU.add), R=[sm_sb], W=[t1])
        S.op("dve", lambda e: e.reciprocal(t1[:, :], t1[:, :]), R=[t1], W=[t1])
        S.op("dve", lambda e: e.tensor_scalar(sm_sb[:, :], sm_sb[:, :], t1[:, 0:1], None, ALU.mult), R=[sm_sb, t1], W=[sm_sb])
        S.op("dve", lambda e: e.tensor_tensor(sm_sb[:, :], sm_sb[:, :], lc_sb[:, :], ALU.mult), R=[sm_sb, lc_sb], W=[sm_sb])
        S.op("dve", lambda e: e.tensor_tensor(t1[:, :], sm_sb[:, 0:1], sm_sb[:, 1:2], ALU.add), R=[sm_sb], W=[t1])
        S.op("dve", lambda e: e.tensor_scalar(oml[:, :], t1[:, :], -1.0, 1.0, ALU.mult, ALU.add), R=[t1], W=[oml])
        S.op("dve", lambda e: e.memset(Sst[:, :], 0.0), W=[Sst])
        S.op("dve", lambda e: e.memset(Sb[:, :], 0.0), W=[Sb])

        for st in range(SEQ // TS):
            b = st % 2
            tsl = slice(st * TS, (st + 1) * TS)
            S.dma("sp", hq_sb[b][:, :], hqT[:, tsl], R=[hqT], W=[hq_sb[b]])
            S.dma("sp", hf_sb[b][:, :], hfT[:, tsl], R=[hfT], W=[hf_sb[b]])
            S.op("act", lambda e: e.activation(keyf[:, :], hf_sb[b][:, :], AF.Sigmoid, scale=-1.0), R=[hf_sb[b]], W=[keyf])
            S.op("dve", lambda e: e.tensor_scalar(keyf[:, :], keyf[:, :], oml[:, 0:1], None, ALU.mult), R=[keyf, oml], W=[keyf])
            S.op("dve", lambda e: e.tensor_scalar(lgf[:, :], keyf[:, :], KEY_MAX, None, ALU.min), R=[keyf], W=[lgf])
            S.op("act", lambda e: e.activation(lgf[:, :], lgf[:, :], AF.Ln, bias=1.0, scale=-1.0), R=[lgf], W=[lgf])
            S.op("dve", lambda e: e.tensor_tensor_scan(A[:, :], rmask[:, :], lgf[:, :], 0.0, ALU.mult, ALU.add), R=[rmask, lgf], W=[A])
            S.op("act", lambda e: e.activation(eA[:, :], A[:, :], AF.Exp), R=[A], W=[eA])
            S.op("act", lambda e: e.activation(eAn[:, :], A[:, :], AF.Exp, scale=-1.0), R=[A], W=[eAn])
            S.op("act", lambda e: e.activation(hq_sb[b][:, :], hq_sb[b][:, :], AF.Silu), R=[hq_sb[b]], W=[hq_sb[b]])
            S.op("dve", lambda e: e.tensor_tensor(qt[:, :], hq_sb[b][:, :], eA[:, :], ALU.mult), R=[hq_sb[b], eA], W=[qt])
            S.op("pool", lambda e: e.tensor_tensor(kt[:, :], keyf[:, :], eAn[:, :], ALU.mult), R=[keyf, eAn], W=[kt])
            for g in range(NCH // 4):
                pt = ptr[g % 2]
                for i in range(4):
                    c = g * 4 + i
                    S.op("pe", lambda e: e.transpose(pt[0:64, i * 128:(i + 1) * 128], kt[:, c * C:(c + 1) * C], ident[:, :]), R=[kt, ident], W=[pt])
                S.op("act", lambda e: e.copy(ktT[:, g * 4:(g + 1) * 4, :], pt[0:64, 0:512].rearrange("p (i d) -> p i d", i=4)), R=[pt], W=[(ktT, g)])
            def pre(c):
                cs = slice(c * C, (c + 1) * C)
                cg = st * NCH + c
                pss = ps_s[c % 2]
                psd = ps_d[c % 2]
                sb_ = smb[c % 2]
                ts_ = tS[c % 2]
                el = eA[:, (c + 1) * C - 1:(c + 1) * C]
                S.op("pe", lambda e: e.matmul(pss[0:64, 0:64], kt[:, cs], qt[:, cs], start=True, stop=True), R=[kt, qt], W=[pss])
                S.op("pe", lambda e: e.matmul(psd[:, 0:64], ktT[:, c, :], v_sb[:, cg, :], start=True, stop=True), R=[(ktT, c // 4), v_sb], W=[psd])
                S.op("dve", lambda e: e.tensor_tensor(sb_[:, :], pss[0:64, 0:64], tri[:, :], ALU.mult), R=[pss, tri], W=[sb_])
                S.op("dve", lambda e: e.tensor_scalar(ts_[:, :], psd[:, 0:64], el, None, ALU.mult), R=[psd, eA], W=[ts_])

            pre(0)
            pre(1)
            for c in range(NCH):
                cs = slice(c * C, (c + 1) * C)
                cg = st * NCH + c
                pso = ps_o[(c // 8) % 2]
                oc = slice((c % 8) * C, (c % 8 + 1) * C)
                sb_ = smb[c % 2]
                ts_ = tS[c % 2]
                el = eA[:, (c + 1) * C - 1:(c + 1) * C]
                S.op("pe", lambda e: e.matmul(pso[0:64, oc], v_sb[:, cg, :], sb_[:, :], start=True, stop=False), R=[v_sb, sb_], W=[pso])
                S.op("pe", lambda e: e.matmul(pso[0:64, oc], Sb[:, :], qt[:, cs], start=False, stop=True), R=[Sb, qt], W=[pso])
                S.op("dve", lambda e: e.scalar_tensor_tensor(Sst[:, :], Sst[:, :], el, ts_[:, :], ALU.mult, ALU.add), R=[Sst, eA, ts_], W=[Sst])
                S.op("act", lambda e: e.copy(Sb[:, :], Sst[:, :]), R=[Sst], W=[Sb])
                if c % 8 == 7:
                    g8 = c // 8
                    S.op("act", lambda e: e.copy(o_sb[b][:, g8 * 512:(g8 + 1) * 512], pso[0:64, :]), R=[pso], W=[o_sb[b]])
                if c + 2 < NCH:
                    pre(c + 2)
            S.dma("sp", oT[:, tsl], o_sb[b][:, :], R=[o_sb[b]], W=[oT])
        S.finish([oT])
    return nc


def build_tail():
    nc = bass.Bass("TRN2", target_bir_lowering=False)
    T = 512
    H = 16
    W_ = T + H
    with ExitStack() as ctx:
        S = Sched(nc, ctx)
        x1h = S.dram("x1h", [D, H + TPC], F32, "ExternalInput")
        ybT = S.dram("ybT", [256, TPC], F32, "ExternalInput")
        oT = S.dram("oT", [512, TPC], F32, "ExternalInput")
        wmix = S.dram("wmix", [D, 6728], F32, "ExternalInput")
        modv = S.dram("modv", [128, 3, 8], F32, "ExternalInput")
        lngd = S.dram("lng", [128, 8], F32, "ExternalInput")
        lnbd = S.dram("lnb", [128, 8], F32, "ExternalInput")
        onesd = S.dram("onesm", [128, 128], F32, "ExternalInput")
        onesrd = S.dram("onesr", [128, 128], F32, "ExternalInput")
        poolwd = S.dram("poolw", [128, 2, 64], F32, "ExternalInput")
        pscaled = S.dram("pscale", [64, 4], F32, "ExternalInput")
        normgd = S.dram("normg", [128, 4], F32, "ExternalInput")
        wbrd = S.dram("wbr", [D, D], F32, "ExternalInput")
        woutd = S.dram("wout", [D, D], F32, "ExternalInput")
        invwd = S.dram("invw", [128, 2], F32, "ExternalInput")
        corrd = S.dram("corr", [128, 2, 16], F32, "ExternalInput")
        hflagd = S.dram("hflag", [128, 1], F32, "ExternalInput")
        x2 = S.dram("x2", [D, TPC], F32, "ExternalOutput")

        def ld(name, shape, src, dt=F32, q="sp"):
            b = S.sbuf(name, shape, dt)
            S.dma(q, b.t[tuple(slice(None) for _ in shape)], src, R=[], W=[b])
            return b
        mod_sb = ld("mod_sb", [128, 3, 8], modv[:, :, :])
        lng = ld("lng_sb", [128, 8], lngd[:, :])
        lnb = ld("lnb_sb", [128, 8], lnbd[:, :])
        onesm = ld("onesm_sb", [128, 128], onesd[:, :])
        onesr = ld("onesr_sb", [128, 128], onesrd[:, :])
        poolw = ld("poolw_sb", [128, 2, 64], poolwd[:, :, :], BF16, "pool")
        pscale = ld("pscale_sb", [64, 4], pscaled[:, :])
        normg = ld("normg_sb", [128, 4], normgd[:, :])
        invw = ld("invw_sb", [128, 2], invwd[:, :])
        corr = ld("corr_sb", [128, 2, 16], corrd[:, :, :])
        hflag = ld("hflag_sb", [128, 1], hflagd[:, :])
        wv = wmix.t.rearrange("(kc p) n -> p kc n", p=128)
        wa = ld("wa", [128, 8, 256], wv[:, :, C_A:C_A + 256], BF16, "pool")
        whg = ld("whg", [128, 8, 512], wv[:, :, C_HG:C_HG + 512], BF16, "pool")
        wbr_a = ld("wbr_a", [64, 4, D], wbrd.t[0:256, :].rearrange("(g p) n -> p g n", p=64), BF16, "pool")
        wbr_b = ld("wbr_b", [128, 2, D], wbrd.t[256:512, :].rearrange("(g p) n -> p g n", p=128), BF16, "pool")
        wbr_c = ld("wbr_c", [128, 4, D], wbrd.t[512:1024, :].rearrange("(g p) n -> p g n", p=128), BF16, "pool")
        wout = ld("wout_sb", [128, 8, D], woutd.t.rearrange("(g p) n -> p g n", p=128), BF16, "pool")
        wg_sb = [S.sbuf("wg_sb%d" % i, [128, 8, 3, 128], BF16) for i in range(3)]
        NG = (TPC // 512) * 8

        def load_wg(gidx):
            d_ = gidx % 8
            wg_ = wg_sb[gidx % 3]
            for i_ in range(3):
                S.dma("pool", wg_[:, :, i_, :], wv[:, :, C_G + i_ * D + d_ * 128:C_G + i_ * D + (d_ + 1) * 128], R=[wmix], W=[wg_])
        sc1 = S.sbuf("sc1", [128, 8], F32)
        g1p = S.sbuf("g1p", [128, 8], F32)
        S.op("dve", lambda e: e.tensor_scalar(sc1[:, :], mod_sb[:, 1, :], 1.0, None, ALU.add), R=[mod_sb], W=[sc1])
        S.op("dve", lambda e: e.tensor_scalar(g1p[:, :], mod_sb[:, 2, :], 1.0, None, ALU.add), R=[mod_sb], W=[g1p])

        x_sb = S.sbuf("x_sb", [128, 8, W_], F32, nsub=8)
        hb = S.sbuf("hb", [128, 8, W_], BF16)
        a_sb = S.sbuf("a_sb", [128, 2, W_], F32)
        s2 = S.sbuf("s2", [128, 2, W_], F32)
        s4 = S.sbuf("s4", [128, 2, W_], F32)
        s8 = S.sbuf("s8", [128, W_], F32)
        pw = S.sbuf("pw", [128, 2, T], F32)
        d_b = S.sbuf("d_b", [128, 2, T], BF16)
        ya_b = S.sbuf("ya_b", [64, 4, T], BF16)
        sg_hg = S.sbuf("sg_hg", [128, 4, T], F32)
        o_sb = S.sbuf("o_sb", [128, 4, T], F32)
        yc_b = S.sbuf("yc_b", [128, 4, T], BF16)
        yb_b = S.sbuf("yb_b", [128, 2, T], BF16)
        g_sb = [S.sbuf("g_sb%d" % i, [128, 3, T], F32) for i in range(2)]
        merged = S.sbuf("merged", [128, 8, T], BF16, nsub=8)
        tmp = ln_tmps(S)
        S.mkbanks(8)
        x2v = _fm(x2.t)
        xv = _fm(x1h.t)
        load_wg(0)
        load_wg(1)
        for tt in range(TPC // T):
            tsl = slice(tt * T, (tt + 1) * T)
            S.dma("sp", x_sb[:, :, :], xv[:, :, tt * T:tt * T + W_], R=[x1h], W=[x_sb])
            S.dma("sp", o_sb[:, :, :], _fm(oT.t)[:, :, tsl], R=[oT], W=[o_sb])
            S.dma("pool", yb_b[:, :, :], _fm(ybT.t)[:, :, tsl], R=[ybT], W=[yb_b])
            for d in range(8):
                S.op("act", lambda e: e.activation(hb[:, d, :], x_sb[:, d, :], AF.Identity, bias=mod_sb[:, 0, d:d + 1], scale=sc1[:, d:d + 1]),
                     R=[(x_sb, d), mod_sb, sc1], W=[hb])
            for d in range(8):
                S.op("pool", lambda e: e.tensor_scalar(x_sb[:, d, H:], x_sb[:, d, H:], ALPHA, 0.0, ALU.mult, ALU.add), R=[(x_sb, d)], W=[(x_sb, d)])
            for c in range(2):
                p = S.bank()
                ph = S.bank()
                for kc in range(8):
                    S.op("pe", lambda e: e.matmul(p[:, :], wa[:, kc, c * 128:(c + 1) * 128], hb[:, kc, H:], start=(kc == 0), stop=(kc == 7)), R=[wa, hb], W=[p])
                for kc in range(8):
                    S.op("pe", lambda e: e.matmul(ph[:, 0:H], wa[:, kc, c * 128:(c + 1) * 128], hb[:, kc, 0:H], start=(kc == 0), stop=(kc == 7)), R=[wa, hb], W=[ph])
                S.op("act", lambda e: e.copy(a_sb[:, c, H:], p[:, :]), R=[p], W=[a_sb])
                if tt == 0:
                    S.op("dve", lambda e: e.tensor_scalar(a_sb[:, c, 0:H], ph[:, 0:H], hflag[:, 0:1], None, ALU.mult), R=[ph, hflag], W=[a_sb])
                else:
                    S.op("dve", lambda e: e.tensor_copy(a_sb[:, c, 0:H], ph[:, 0:H]), R=[ph], W=[a_sb])
            S.op("dve", lambda e: e.tensor_tensor(s2[:, :, 1:W_], a_sb[:, :, 1:W_], a_sb[:, :, 0:W_ - 1], ALU.add), R=[a_sb], W=[s2])
            S.op("pool", lambda e: e.tensor_tensor(s4[:, :, 3:W_], s2[:, :, 3:W_], s2[:, :, 1:W_ - 2], ALU.add), R=[s2], W=[s4])
            S.op("dve", lambda e: e.tensor_tensor(s8[:, 7:W_], s4[:, 1, 7:W_], s4[:, 1, 3:W_ - 4], ALU.add), R=[s4], W=[s8])
            S.op("pool", lambda e: e.tensor_copy(pw[0:64, 0, :], s2[0:64, 0, H:]), R=[s2], W=[pw])
            S.op("pool", lambda e: e.tensor_copy(pw[64:128, 0, :], s4[64:128, 0, H:]), R=[s4], W=[pw])
            S.op("pool", lambda e: e.tensor_copy(pw[0:64, 1, :], s8[0:64, H:]), R=[s8], W=[pw])
            S.op("dve", lambda e: e.tensor_tensor(pw[64:128, 1, :], s8[64:128, H:], s8[64:128, H - 8:W_ - 8], ALU.add), R=[s8], W=[pw])
            if tt == 0:
                S.op("dve", lambda e: e.tensor_tensor(pw[:, :, 0:16], pw[:, :, 0:16], corr[:, :, :], ALU.mult), R=[pw, corr], W=[pw])
            for c in range(2):
                S.op("dve", lambda e: e.scalar_tensor_tensor(d_b[:, c, :], pw[:, c, :], invw[:, c:c + 1], a_sb[:, c, H:], ALU.mult, ALU.subtract),
                     R=[pw, invw, a_sb], W=[d_b])
            for g in range(4):
                p = S.bank()
                ps_ = slice((g % 2) * 64, (g % 2) * 64 + 64)
                S.op("pe", lambda e: e.matmul(p[0:64, :], poolw[ps_, g // 2, :], d_b[ps_, g // 2, :], start=True, stop=True), R=[poolw, d_b], W=[p])
                S.op("dve", lambda e: e.tensor_scalar(ya_b[:, g, :], p[0:64, :], pscale[:, g:g + 1], None, ALU.mult), R=[p, pscale], W=[ya_b])
            for h in range(4):
                p = S.bank()
                for kc in range(8):
                    S.op("pe", lambda e: e.matmul(p[:, :], whg[:, kc, h * 128:(h + 1) * 128], hb[:, kc, H:], start=(kc == 0), stop=(kc == 7)), R=[whg, hb], W=[p])
                S.op("act", lambda e: e.activation(sg_hg[:, h, :], p[:, :], AF.Silu), R=[p], W=[sg_hg])
            for h in range(4):
                sq = tmp["sq"][h % 2]
                S.op("act", lambda e: e.activation(sq[:, :], o_sb[:, h, :], AF.Square), R=[o_sb], W=[sq])
                pr = S.bank()
                S.op("pe", lambda e: e.matmul(pr[:, :], onesr[:, :], sq[:, :], start=True, stop=True), R=[onesr, sq], W=[pr])
                rs = tmp["rstd"]
                S.op("act", lambda e: e.activation(rs[:, :], pr[:, :], AF.Sqrt, bias=RMS_EPS, scale=1.0), R=[pr], W=[rs])
                S.op("dve", lambda e: e.reciprocal(rs[:, :], rs[:, :]), R=[rs], W=[rs])
                S.op("dve", lambda e: e.tensor_tensor(o_sb[:, h, :], o_sb[:, h, :], rs[:, :], ALU.mult), R=[o_sb, rs], W=[o_sb])
                S.op("dve", lambda e: e.scalar_tensor_tensor(yc_b[:, h, :], o_sb[:, h, :], normg[:, h:h + 1], sg_hg[:, h, :], ALU.mult, ALU.mult),
                     R=[o_sb, normg, sg_hg], W=[yc_b])
            for d in range(8):
                dsl = slice(d * 128, (d + 1) * 128)
                gidx = tt * 8 + d
                wg = wg_sb[gidx % 3]
                gs = g_sb[d % 2]
                if gidx + 2 < NG:
                    load_wg(gidx + 2)
                for i in range(3):
                    p = S.bank()
                    for kc in range(8):
                        S.op("pe", lambda e: e.matmul(p[:, :], wg[:, kc, i, :], hb[:, kc, H:], start=(kc == 0), stop=(kc == 7)), R=[wg, hb], W=[p])
                    S.op("act", lambda e: e.activation(gs[:, i, :], p[:, :], AF.Sigmoid), R=[p], W=[gs])
                pa = S.bank()
                for g in range(4):
                    S.op("pe", lambda e: e.matmul(pa[:, :], wbr_a[:, g, dsl], ya_b[:, g, :], start=(g == 0), stop=(g == 3)), R=[wbr_a, ya_b], W=[pa])
                pb = S.bank()
                for g in range(2):
                    S.op("pe", lambda e: e.matmul(pb[:, :], wbr_b[:, g, dsl], yb_b[:, g, :], start=(g == 0), stop=(g == 1)), R=[wbr_b, yb_b], W=[pb])
                pc = S.bank()
                for g in range(4):
                    S.op("pe", lambda e: e.matmul(pc[:, :], wbr_c[:, g, dsl], yc_b[:, g, :], start=(g == 0), stop=(g == 3)), R=[wbr_c, yc_b], W=[pc])
                t0, t1, t2 = tmp["t"][0], tmp["t"][1], tmp["m2"]
                S.op("dve", lambda e: e.tensor_tensor(t0[:, :], pa[:, :], gs[:, 0, :], ALU.mult), R=[pa, gs], W=[t0])
                S.op("dve", lambda e: e.tensor_tensor(t1[:, :], pb[:, :], gs[:, 1, :], ALU.mult), R=[pb, gs], W=[t1])
                S.op("dve", lambda e: e.tensor_tensor(t2[:, :], pc[:, :], gs[:, 2, :], ALU.mult), R=[pc, gs], W=[t2])
                S.op("pool", lambda e: e.tensor_tensor(t0[:, :], t0[:, :], t1[:, :], ALU.add), R=[t0, t1], W=[t0])
                S.op("pool", lambda e: e.tensor_tensor(merged[:, d, :], t0[:, :], t2[:, :], ALU.add), R=[t0, t2], W=[(merged, d)])
            for d in range(8):
                p = S.bank()
                for kc in range(8):
                    S.op("pe", lambda e: e.matmul(p[:, :], wout[:, kc, d * 128:(d + 1) * 128], merged[:, kc, :], start=(kc == 0), stop=(kc == 7)),
                         R=[wout, (merged, kc)], W=[p])
                S.op("dve", lambda e: e.scalar_tensor_tensor(x_sb[:, d, H:], p[:, :], g1p[:, d:d + 1], x_sb[:, d, H:], ALU.mult, ALU.add),
                     R=[p, g1p, (x_sb, d)], W=[(x_sb, d)])
            emit_ln(S, x_sb, lambda d: d, H, lng, lnb, onesm, tmp)
            S.dma("sp", x2v[:, :, tsl], x_sb[:, :, H:], R=[x_sb], W=[x2])
        S.finish([x2])
    return nc


def _cm(v):
    v = np.asarray(v)
    return np.ascontiguousarray(v.reshape(-1, 128).T)


def _modv(mod, j):
    return np.ascontiguousarray(mod[j].reshape(3, 8, 128).transpose(2, 0, 1))


_ONES_LN = np.full((128, 128), 1.0 / 1024, np.float32)
_ONES_RMS = np.full((128, 128), 1.0 / 128, np.float32)


def mod_inputs(inp):
    ins = []
    for c in range(NCORE):
        wsl = np.stack([inp['w_ada'][l][:, j * 128:(j + 1) * 128] for l in range(2) for j in range(c, 72, 8)])
        bsl = np.stack([inp['b_ada'][l][j * 128:(j + 1) * 128] for l in range(2) for j in range(c, 72, 8)], axis=1)
        ins.append({'cT': _cm(inp['c'][0]), 'wsl': np.ascontiguousarray(wsl), 'bsl': np.ascontiguousarray(bsl)})
    return ins


def mod_gather(results):
    mods = np.zeros((2, 9216), np.float32)
    for c in range(NCORE):
        o = results[c]['modo']
        for l in range(2):
            for i, j in enumerate(range(c, 72, 8)):
                mods[l, j * 128:(j + 1) * 128] = o[:, l * 9 + i]
    return mods.reshape(2, 3, 3, 1024)


def ffn_inputs(l, which, xT, mod, inp):
    sub = 0 if which == 0 else 2
    ins = []
    for c in range(NCORE):
        ins.append({'xT': np.ascontiguousarray(xT[:, c * TPC:(c + 1) * TPC]), 'modv': _modv(mod, sub),
                    'lng': _cm(inp['ln_g'][l, sub]), 'lnb': _cm(inp['ln_b'][l, sub]),
                    'w_in': inp['ffn_w_in'][l, which], 'w_out': inp['ffn_w_out'][l, which], 'onesm': _ONES_LN})
    return ins


def proj_inputs(l, x1T, mod, inp):
    return [{'xT': np.ascontiguousarray(x1T[:, c * TPC:(c + 1) * TPC]), 'modv': _modv(mod, 1), 'wmix': inp['mix_w_in'][l]} for c in range(NCORE)]


def hg_inputs(l, hqT, hfT, hitm, inp):
    ins = []
    rm = np.ones((128, 2048), np.float32)
    rm[:, ::64] = 0
    lc = np.zeros((128, 2), np.float32)
    lc[:, 1:l + 1] = 1.0
    tri = np.triu(np.ones((64, 64), np.float32))
    idb = np.eye(128).astype(NPBF)
    for c in range(NCORE):
        h, vh = c // 2, c % 2
        ins.append({'hqT': np.ascontiguousarray(hqT[h * 128:(h + 1) * 128]), 'hfT': np.ascontiguousarray(hfT[h * 128:(h + 1) * 128]),
                    'vtm': np.ascontiguousarray(hitm[:, h * 128 + vh * 64:h * 128 + (vh + 1) * 64]),
                    'lbraw': np.ascontiguousarray(inp['hgrn_lb'][:, h * 128:(h + 1) * 128].T), 'lcoef': lc,
                    'triT': tri, 'identb': idb, 'rmask': rm})
    return ins


def hg_gather(results):
    oT = np.zeros((512, SEQ), np.float32)
    for c in range(NCORE):
        h, vh = c // 2, c % 2
        oT[h * 128 + vh * 64:h * 128 + (vh + 1) * 64] = results[c]['oT']
    return oT


def tail_inputs(l, x1T, ybT, oT, mod, inp):
    ins = []
    pw = inp['pool_w'][l]
    poolw = np.zeros((128, 2, 64), np.float32)
    for g in range(4):
        poolw[(g % 2) * 64:(g % 2) * 64 + 64, g // 2, :] = pw[g]
    invw = np.zeros((128, 2), np.float32)
    invw[:64, 0], invw[64:, 0], invw[:64, 1], invw[64:, 1] = 1 / 2, 1 / 4, 1 / 8, 1 / 16
    wins = {(0, 0): 2, (1, 0): 4, (0, 1): 8, (1, 1): 16}
    for c in range(NCORE):
        x1h = np.zeros((D, 16 + TPC), np.float32)
        if c > 0:
            x1h[:, :] = x1T[:, c * TPC - 16:(c + 1) * TPC]
        else:
            x1h[:, 16:] = x1T[:, :TPC]
        corr = np.ones((128, 2, 16), np.float32)
        if c == 0:
            for (ph, ch), w in wins.items():
                for t in range(16):
                    corr[ph * 64:(ph + 1) * 64, ch, t] = w / min(t + 1, w)
        ins.append({'x1h': x1h, 'ybT': np.ascontiguousarray(ybT[:, c * TPC:(c + 1) * TPC]), 'oT': np.ascontiguousarray(oT[:, c * TPC:(c + 1) * TPC]),
                    'wmix': inp['mix_w_in'][l], 'modv': _modv(mod, 1), 'lng': _cm(inp['ln_g'][l, 1]), 'lnb': _cm(inp['ln_b'][l, 1]),
                    'onesm': _ONES_LN, 'onesr': _ONES_RMS, 'poolw': poolw, 'pscale': np.ascontiguousarray(inp['pool_scale'][l].reshape(4, 64).T),
                    'normg': _cm(inp['hgrn_norm_g'][l]), 'wbr': inp['w_branch'][l], 'wout': inp['w_out'][l], 'invw': invw, 'corr': corr,
                    'hflag': np.full((128, 1), 0.0 if c == 0 else 1.0, np.float32)})
    return ins


NIT = 26
NBLK = 16


def build_att(nblk=NBLK, nit=NIT):
    nc = bass.Bass("TRN2", target_bir_lowering=False)
    with ExitStack() as ctx:
        S = Sched(nc, ctx)
        kiTd = S.dram("kiT", [64, SEQ], BF16, "ExternalInput")
        kThd = S.dram("kTh", [64, 4, SEQ], BF16, "ExternalInput")
        vaugd = S.dram("vaug", [SEQ, 260], BF16, "ExternalInput")
        qThd = S.dram("qTh", [64, 4, TPC], BF16, "ExternalInput")
        qiThd = S.dram("qiTh", [64, 8, TPC], BF16, "ExternalInput")
        wid = S.dram("wi", [128, NBLK, 8], F32, "ExternalInput")
        BTd = S.dram("BT", [128, 4, 1536], F32, "ExternalInput")
        b31d = S.dram("b31", [128, 4], F32, "ExternalInput")
        relbd = S.dram("relbT", [128, 4, 32], F32, "ExternalInput")
        pend = S.dram("pen", [128, 1024], F32, "ExternalInput")
        identd = S.dram("identb", [128, 128], BF16, "ExternalInput")
        ones64d = S.dram("ones64", [64, 128], BF16, "ExternalInput")
        yb = S.dram("yb", [TPC, 256], F32, "ExternalOutput")

        def ld(name, shape, src, dt=F32, q="sp"):
            b = S.sbuf(name, shape, dt)
            S.dma(q, b.t[tuple(slice(None) for _ in shape)], src, R=[], W=[b])
            return b
        kiT = S.sbuf("kiT_sb", [64, SEQ], BF16)
        for i in range(4):
            S.dma("sp", kiT[:, i * 4096:(i + 1) * 4096], kiTd[:, i * 4096:(i + 1) * 4096], R=[], W=[kiT])
        wi_sb = ld("wi_sb", [128, NBLK, 8], wid[:, :, :])
        BT = ld("BT_sb", [128, 4, 1536], BTd[:, :, :])
        b31 = ld("b31_sb", [128, 4], b31d[:, :])
        relb = ld("relb_sb", [128, 4, 32], relbd[:, :, :])
        pen = ld("pen_sb", [128, 1024], pend[:, :])
        ident = ld("ident_sb", [128, 128], identd[:, :], BF16)
        ones64 = ld("ones64_sb", [64, 128], ones64d[:, :], BF16)

        score = S.sbuf("score", [128, SEQ], F32, nsub=32)
        junk = S.sbuf("junk", [128, 4096], BF16)
        kbuf = [S.sbuf("kbuf%d" % i, [64, 4, 512], BF16) for i in range(2)]
        vbuf = [S.sbuf("vbuf%d" % i, [128, 4, 260], BF16) for i in range(2)]
        ksq = S.sbuf("ksq", [64, 4, 512], BF16)
        q_sb = [S.sbuf("q_sb%d" % i, [64, 4, 128], BF16) for i in range(2)]
        qi_sb = [S.sbuf("qi_sb%d" % i, [64, 8, 128], BF16) for i in range(2)]
        qsq = S.sbuf("qsq", [64, 4, 128], BF16)
        tmpl = S.sbuf("tmpl", [128, 512], F32)
        ebuf = [S.sbuf("ebuf%d" % i, [128, 512], BF16) for i in range(2)]
        mask = [S.sbuf("mask%d" % i, [128, 512], BF16) for i in range(2)]
        pbuf = [S.sbuf("pbuf%d" % i, [128, 512], BF16) for i in range(2)]
        pT = [S.sbuf("pT%d" % i, [128, 512], BF16) for i in range(2)]
        O_sb = S.sbuf("O_sb", [128, 260], F32)
        y_sb = [S.sbuf("y_sb%d" % i, [128, 4, 64], F32) for i in range(2)]
        km2 = S.sbuf("km2", [128, 4], F32)
        red4 = S.sbuf("red4", [128, 4], F32)
        bmax = S.sbuf("bmax", [128, 4], F32)
        wabs = S.sbuf("wabs", [128, 8], F32)
        wsgn = S.sbuf("wsgn", [128, 8], F32)
        mm = S.sbuf("mm", [128, 4], F32)
        nm = S.sbuf("nm", [128, 4], F32)
        nmf = S.sbuf("nmf", [128, 4], F32)
        rec = S.sbuf("rec", [128, 4], F32)
        sm = {n: S.sbuf("bs_" + n, [128, 1], F32) for n in ["A", "lo", "hi", "mid", "cA", "cB", "ge", "dl", "dh"]}
        S.mkbanks(6)
        ptb = [S.psum("ptb%d" % i, [128, 1024], BF16) for i in range(2)]
        onescol = ones64

        S.op("dve", lambda e: e.memset(km2[:, :], 0.0), W=[km2])
        S.op("dve", lambda e: e.tensor_reduce(bmax[:, :], relb[:, :, :], AX.X, ALU.max), R=[relb], W=[bmax])
        for kt in range(SEQ // 512):
            ks = slice(kt * 512, (kt + 1) * 512)
            kb = kbuf[kt % 2]
            S.dma("sp", kb[:, :, :], kThd[:, :, ks], R=[], W=[kb])
            S.op("act", lambda e: e.activation(ksq[:, :, :], kb[:, :, :], AF.Square), R=[kb], W=[ksq])
            for h in range(4):
                p = S.bank()
                S.op("pe", lambda e: e.matmul(p[:, :], ones64[:, :], ksq[:, h, :], start=True, stop=True), R=[ones64, ksq], W=[p])
                S.op("dve", lambda e: e.tensor_reduce(red4[:, h:h + 1], p[:, :], AX.X, ALU.max), R=[p], W=[red4])
            S.op("dve", lambda e: e.tensor_tensor(km2[:, :], km2[:, :], red4[:, :], ALU.max), R=[km2, red4], W=[km2])

        kvi = 0
        for j in range(nblk):
            nk = 1024 * (j + 1)
            nt = nk // 512
            qb = q_sb[j % 2]
            qib = qi_sb[j % 2]
            bsl = slice(j * 128, (j + 1) * 128)
            S.dma("sp", qb[:, :, :], qThd[:, :, bsl], R=[], W=[qb])
            S.dma("sp", qib[:, :, :], qiThd[:, :, bsl], R=[], W=[qib])
            S.op("act", lambda e: e.activation(wsgn[:, :], wi_sb[:, j, :], AF.Sign), R=[wi_sb], W=[wsgn])
            S.op("dve", lambda e: e.tensor_tensor(wabs[:, :], wi_sb[:, j, :], wsgn[:, :], ALU.mult), R=[wi_sb, wsgn], W=[wabs])
            S.op("act", lambda e: e.activation(qsq[:, :, :], qb[:, :, :], AF.Square), R=[qb], W=[qsq])
            pq = S.bank()
            for h in range(4):
                S.op("pe", lambda e: e.matmul(pq[:, 2 * h:2 * h + 2], qsq[:, h, :], onescol[:, 0:2], start=True, stop=True), R=[qsq, ones64], W=[pq])
            S.op("dve", lambda e: e.tensor_tensor(mm[:, :], pq[:, 0:8].rearrange("p (h two) -> p h two", two=2)[:, :, 0], km2[:, :], ALU.mult), R=[pq, km2], W=[mm])
            S.op("act", lambda e: e.activation(mm[:, :], mm[:, :], AF.Sqrt), R=[mm], W=[mm])
            S.op("dve", lambda e: e.scalar_tensor_tensor(mm[:, :], mm[:, :], 1.05 * 0.125, bmax[:, :], ALU.mult, ALU.add), R=[mm, bmax], W=[mm])
            S.op("dve", lambda e: e.tensor_scalar(nm[:, :], mm[:, :], -1.0, None, ALU.mult), R=[mm], W=[nm])
            S.op("dve", lambda e: e.tensor_tensor(nmf[:, :], b31[:, :], mm[:, :], ALU.subtract), R=[b31, mm], W=[nmf])
            for kt in range(nt):
                ks = slice(kt * 512, (kt + 1) * 512)
                for h in range(8):
                    p = S.bank()
                    S.op("pe", lambda e: e.matmul(p[:, :], qib[:, h, :], kiT[:, ks], start=True, stop=True), R=[qib, kiT], W=[p])
                    S.op("act", lambda e: e.activation(p[:, :], p[:, :], AF.Relu, scale=wabs[:, h:h + 1]), R=[p, wabs], W=[p])
                    if h == 0:
                        S.op("dve", lambda e: e.tensor_scalar(score[:, ks], p[:, :], wsgn[:, 0:1], None, ALU.mult), R=[p, wsgn], W=[(score, kt)])
                    else:
                        S.op("dve", lambda e: e.scalar_tensor_tensor(score[:, ks], p[:, :], wsgn[:, h:h + 1], score[:, ks], ALU.mult, ALU.add),
                             R=[p, wsgn, (score, kt)], W=[(score, kt)])
            allr = [(score, list(range(nt)))]
            S.op("dve", lambda e: e.tensor_reduce(sm["A"][:, :], score[:, 0:nk], AX.X, ALU.max, apply_absolute_value=True), R=allr, W=[sm["A"]])
            S.op("dve", lambda e: e.tensor_tensor(score[:, nk - 1024:nk], score[:, nk - 1024:nk], pen[:, :], ALU.add),
                 R=[(score, [nt - 2, nt - 1]), pen], W=[(score, [nt - 2, nt - 1])])
            S.op("dve", lambda e: e.tensor_scalar(sm["hi"][:, :], sm["A"][:, :], 1.01, 1.0, ALU.mult, ALU.add), R=[sm["A"]], W=[sm["hi"]])
            S.op("dve", lambda e: e.tensor_scalar(sm["lo"][:, :], sm["hi"][:, :], -1.0, None, ALU.mult), R=[sm["hi"]], W=[sm["lo"]])
            nch = (nk + 4095) // 4096
            for it in range(nit):
                S.op("dve", lambda e: e.tensor_tensor(sm["mid"][:, :], sm["lo"][:, :], sm["hi"][:, :], ALU.add), R=[sm["lo"], sm["hi"]], W=[sm["mid"]])
                S.op("dve", lambda e: e.tensor_scalar(sm["mid"][:, :], sm["mid"][:, :], 0.5, None, ALU.mult), R=[sm["mid"]], W=[sm["mid"]])
                cur = None
                for ch in range(nch):
                    c0 = ch * 4096
                    n = min(4096, nk - c0)
                    dst = sm["cA"] if ch % 2 == 0 else sm["cB"]
                    rr = [(score, list(range(c0 // 512, (c0 + n) // 512))), sm["mid"]]
                    if cur is None:
                        S.op("dve", lambda e: e.tensor_scalar(junk[:, 0:n], score[:, c0:c0 + n], sm["mid"][:, 0:1], None, ALU.is_ge, ALU.add, accum_out=dst[:, 0:1]),
                             R=rr, W=[dst])
                    else:
                        S.op("dve", lambda e: e.tensor_scalar(junk[:, 0:n], score[:, c0:c0 + n], sm["mid"][:, 0:1], cur[:, 0:1], ALU.is_ge, ALU.add, accum_out=dst[:, 0:1]),
                             R=rr + [cur], W=[dst])
                    cur = dst
                S.op("dve", lambda e: e.tensor_scalar(sm["ge"][:, :], cur[:, :], TOPK - 0.5, None, ALU.is_ge), R=[cur], W=[sm["ge"]])
                S.op("dve", lambda e: e.tensor_tensor(sm["dl"][:, :], sm["mid"][:, :], sm["lo"][:, :], ALU.subtract), R=[sm["mid"], sm["lo"]], W=[sm["dl"]])
                S.op("dve", lambda e: e.tensor_tensor(sm["dh"][:, :], sm["hi"][:, :], sm["mid"][:, :], ALU.subtract), R=[sm["mid"], sm["hi"]], W=[sm["dh"]])
                S.op("dve", lambda e: e.scalar_tensor_tensor(sm["lo"][:, :], sm["dl"][:, :], sm["ge"][:, 0:1], sm["lo"][:, :], ALU.mult, ALU.add),
                     R=[sm["dl"], sm["ge"], sm["lo"]], W=[sm["lo"]])
                S.op("dve", lambda e: e.scalar_tensor_tensor(sm["hi"][:, :], sm["dh"][:, :], sm["ge"][:, 0:1], sm["mid"][:, :], ALU.mult, ALU.add),
                     R=[sm["dh"], sm["ge"], sm["mid"]], W=[sm["hi"]])
            thr = sm["lo"]
            S.op("pool", lambda e: e.memset(O_sb[:, :], 0.0), W=[O_sb])
            cnt2 = 0
            for kt in range(nt):
                ks = slice(kt * 512, (kt + 1) * 512)
                kb = kbuf[kvi % 2]
                vb = vbuf[kvi % 2]
                kvi += 1
                S.dma("sp", kb[:, :, :], kThd[:, :, ks], R=[], W=[kb])
                S.dma("sp", vb[:, :, :], vaugd.t.rearrange("(i p) f -> p i f", p=128)[:, kt * 4:(kt + 1) * 4, :], R=[], W=[vb])
                mk = mask[kt % 2]
                S.op("dve", lambda e: e.tensor_scalar(mk[:, :], score[:, ks], thr[:, 0:1], None, ALU.is_ge), R=[(score, kt), thr], W=[mk])
                inwin = kt >= nt - 3
                wcol = (kt - (nt - 3)) * 512
                po = S.bank()
                for h in range(4):
                    pl = S.bank()
                    S.op("pe", lambda e: e.matmul(pl[:, :], qb[:, h, :], kb[:, h, :], start=True, stop=True), R=[qb, kb], W=[pl])
                    eb = ebuf[cnt2 % 2]
                    pb = pbuf[cnt2 % 2]
                    ptt = ptb[cnt2 % 2]
                    pTt = pT[cnt2 % 2]
                    if inwin:
                        S.op("dve", lambda e: e.scalar_tensor_tensor(tmpl[:, :], pl[:, :], 0.125, BT[:, h, wcol:wcol + 512], ALU.mult, ALU.add), R=[pl, BT], W=[tmpl])
                        S.op("act", lambda e: e.activation(eb[:, :], tmpl[:, :], AF.Exp, bias=nm[:, h:h + 1], scale=1.0), R=[tmpl, nm], W=[eb])
                    else:
                        S.op("act", lambda e: e.activation(eb[:, :], pl[:, :], AF.Exp, bias=nmf[:, h:h + 1], scale=0.125), R=[pl, nmf], W=[eb])
                    S.op("pool", lambda e: e.tensor_tensor(pb[:, :], eb[:, :], mk[:, :], ALU.mult), R=[eb, mk], W=[pb])
                    for i in range(4):
                        S.op("pe", lambda e: e.transpose(ptt[:, i * 128:(i + 1) * 128], pb[:, i * 128:(i + 1) * 128], ident[:, :]), R=[pb, ident], W=[ptt])
                    if cnt2 % 2 == 0:
                        S.op("dve", lambda e: e.tensor_copy(pTt[:, :], ptt[:, 0:512]), R=[ptt], W=[pTt])
                    else:
                        S.op("act", lambda e: e.copy(pTt[:, :], ptt[:, 0:512]), R=[ptt], W=[pTt])
                    for i in range(4):
                        S.op("pe", lambda e: e.matmul(po[:, h * 65:(h + 1) * 65], pTt[:, i * 128:(i + 1) * 128], vb[:, i, h * 65:(h + 1) * 65], start=(i == 0), stop=(i == 3)),
                             R=[pTt, vb], W=[po])
                    cnt2 += 1
                S.op("dve", lambda e: e.tensor_tensor(O_sb[:, :], O_sb[:, :], po[:, 0:260], ALU.add), R=[O_sb, po], W=[O_sb])
            Ov = O_sb[:, :].rearrange("p (h e) -> p h e", e=65)
            S.op("dve", lambda e: e.reciprocal(rec[:, :], Ov[:, :, 64]), R=[O_sb], W=[rec])
            ys = y_sb[j % 2]
            for h in range(4):
                S.op("dve", lambda e: e.tensor_scalar(ys[:, h, :], Ov[:, h, 0:64], rec[:, h:h + 1], None, ALU.mult), R=[O_sb, rec], W=[ys])
            S.dma("sp", yb[bsl, :], ys[:, :, :].rearrange("p h e -> p (h e)"), R=[ys], W=[yb])
        S.finish([yb])
    return nc


def t5_bucket_np(dist):
    dist = np.asarray(dist)
    d32 = np.maximum(dist, 1).astype(np.float32)
    large = 16 + (np.log(d32 / 16) / np.log(128 / 16) * 16).astype(np.int32)
    large = np.minimum(large, 31)
    return np.where(dist < 16, dist, large)


def att_inputs(kiT, kT, vaug, qT, qiT, witm, inp):
    rb = inp['rel_bias']
    kTh = np.ascontiguousarray(kT.reshape(4, 64, SEQ).transpose(1, 0, 2))
    idb = np.eye(128).astype(NPBF)
    ones64 = np.ones((64, 128), NPBF)
    b31 = np.ascontiguousarray(np.broadcast_to(rb[31][None, :], (128, 4))).astype(np.float32)
    relbT = np.ascontiguousarray(np.broadcast_to(rb.T[None], (128, 4, 32))).astype(np.float32)
    ins = []
    sp = np.arange(1024)[None, :]
    pow2 = np.ascontiguousarray(np.broadcast_to((2.0 ** -(np.arange(32) + 1.0))[None, :], (128, 32))).astype(np.float32)
    for c in range(NCORE):
        blocks = [8 * j + c for j in range(NBLK)]
        cols = np.concatenate([np.arange(b * 128, (b + 1) * 128) for b in blocks])
        qTh = np.ascontiguousarray(qT[:, cols].reshape(4, 64, TPC).transpose(1, 0, 2))
        qiTh = np.ascontiguousarray(qiT[:, cols].reshape(8, 64, TPC).transpose(1, 0, 2))
        wi = np.ascontiguousarray(witm[cols].reshape(NBLK, 128, 8).transpose(1, 0, 2))
        tpos = (c * 128 + np.arange(128))[:, None]
        dist = np.maximum(512 + tpos - np.arange(1536)[None, :], 0)
        bk = t5_bucket_np(dist)
        BT = np.ascontiguousarray(rb[bk].transpose(0, 2, 1)).astype(np.float32)
        pen = np.where(sp > tpos, np.float32(-1e30), np.float32(0)).astype(np.float32)
        ins.append({'kiT': kiT, 'kTh': kTh, 'vaug': vaug, 'qTh': qTh, 'qiTh': qiTh, 'wi': wi, 'BT': BT, 'b31': b31, 'relbT': relbT,
                    'pen': pen, 'identb': idb, 'ones64': ones64, 'pow2': pow2})
    return ins


def att_gather(results):
    yb = np.zeros((SEQ, 256), np.float32)
    for c in range(NCORE):
        o = results[c]['yb']
        for j in range(NBLK):
            b = 8 * j + c
            yb[b * 128:(b + 1) * 128] = o[j * 128:(j + 1) * 128]
    return yb


_PROGS = {}


def _prog(name):
    if name not in _PROGS:
        _PROGS[name] = {"mod": build_mod, "ffn": build_ffn, "proj": build_proj, "hg": build_hg, "att": build_att2, "att1": build_att, "tail": build_tail}[name]()
    return _PROGS[name]


def _run(name, ins):
    res = run_bass_kernel_spmd(_prog(name), ins, core_ids=list(range(NCORE)))
    return res.results


def _cat(results, name, axis):
    return np.ascontiguousarray(np.concatenate([results[c][name] for c in range(NCORE)], axis=axis))


def kernel(x, c, w_ada, b_ada, ln_g, ln_b, ffn_w_in, ffn_w_out, mix_w_in, pool_w, pool_scale,
           rel_bias, hgrn_lb, hgrn_norm_g, w_branch, w_out):
    inp = dict(x=x, c=c, w_ada=w_ada, b_ada=b_ada, ln_g=ln_g, ln_b=ln_b, ffn_w_in=ffn_w_in, ffn_w_out=ffn_w_out,
               mix_w_in=mix_w_in, pool_w=pool_w, pool_scale=pool_scale, rel_bias=rel_bias, hgrn_lb=hgrn_lb,
               hgrn_norm_g=hgrn_norm_g, w_branch=w_branch, w_out=w_out)
    inp = {k: np.ascontiguousarray(np.asarray(v, np.float32)) for k, v in inp.items()}
    mods = mod_gather(_run("mod", mod_inputs(inp)))
    xT = np.ascontiguousarray(inp['x'][0].T)
    for l in range(DEPTH):
        mod = mods[l]
        r = _run("ffn", ffn_inputs(l, 0, xT, mod, inp))
        x1T = _cat(r, 'xo', 1)
        r = _run("proj", proj_inputs(l, x1T, mod, inp))
        qT, kT, qiT, kiT = (_cat(r, n, 1) for n in ('qT', 'kT', 'qiT', 'kiT'))
        hqT, hfT = _cat(r, 'hqT', 1), _cat(r, 'hfT', 1)
        vaug, hitm, witm = _cat(r, 'vaug', 0), _cat(r, 'hitm', 0), _cat(r, 'witm', 0)
        r = _run("att", att_inputs(kiT, kT, vaug, qT, qiT, witm, inp))
        ybT = np.ascontiguousarray(att_gather(r).T)
        r = _run("hg", hg_inputs(l, hqT, hfT, hitm, inp))
        oT = hg_gather(r)
        r = _run("tail", tail_inputs(l, x1T, ybT, oT, mod, inp))
        x2T = _cat(r, 'x2', 1)
        r = _run("ffn", ffn_inputs(l, 1, x2T, mod, inp))
        xT = _cat(r, 'xo', 1)
    return np.ascontiguousarray(xT.T)[None].astype(np.float32)


FP8 = mybir.dt.float8e4


def build_att2(nblk=NBLK, nit=NIT):
    nc = bass.Bass("TRN2", target_bir_lowering=False)
    with ExitStack() as ctx:
        S = Sched(nc, ctx)
        kiTd = S.dram("kiT", [64, SEQ], BF16, "ExternalInput")
        kThd = S.dram("kTh", [64, 4, SEQ], BF16, "ExternalInput")
        vaugd = S.dram("vaug", [SEQ, 260], BF16, "ExternalInput")
        qThd = S.dram("qTh", [64, 4, TPC], BF16, "ExternalInput")
        qiThd = S.dram("qiTh", [64, 8, TPC], BF16, "ExternalInput")
        wid = S.dram("wi", [128, NBLK, 8], F32, "ExternalInput")
        BTd = S.dram("BT", [128, 4, 1536], F32, "ExternalInput")
        b31d = S.dram("b31", [128, 4], F32, "ExternalInput")
        relbd = S.dram("relbT", [128, 4, 32], F32, "ExternalInput")
        pend = S.dram("pen", [128, 1024], F32, "ExternalInput")
        identd = S.dram("identb", [128, 128], BF16, "ExternalInput")
        ones64d = S.dram("ones64", [64, 128], BF16, "ExternalInput")
        pow2d = S.dram("pow2", [128, 32], F32, "ExternalInput")
        yb = S.dram("yb", [TPC, 256], F32, "ExternalOutput")

        def ld(name, shape, src, dt=F32, q="sp"):
            b = S.sbuf(name, shape, dt)
            S.dma(q, b.t[tuple(slice(None) for _ in shape)], src, R=[], W=[b])
            return b
        wi_sb = ld("wi_sb", [128, NBLK, 8], wid[:, :, :])
        BT = ld("BT_sb", [128, 4, 1536], BTd[:, :, :])
        b31 = ld("b31_sb", [128, 4], b31d[:, :])
        relb = ld("relb_sb", [128, 4, 32], relbd[:, :, :])
        pen = ld("pen_sb", [128, 1024], pend[:, :])
        ident = ld("ident_sb", [128, 128], identd[:, :], BF16)
        ones64 = ld("ones64_sb", [64, 128], ones64d[:, :], BF16)
        pow2 = ld("pow2_sb", [128, 32], pow2d[:, :])

        score = S.sbuf("score", [128, SEQ], F32, nsub=32)
        maskb = S.sbuf("maskb", [128, SEQ], FP8, nsub=32)
        cand = S.sbuf("cand", [128, 2048], F32, nsub=16)
        junk = S.sbuf("junk", [128, 2048], BF16)
        kib = [S.sbuf("kib%d" % i, [64, 512], BF16) for i in range(3)]
        kbuf = [S.sbuf("kbuf%d" % i, [64, 4, 512], BF16) for i in range(3)]
        vbuf = [S.sbuf("vbuf%d" % i, [128, 4, 260], BF16) for i in range(3)]
        ksq = S.sbuf("ksq", [64, 4, 512], BF16)
        q_sb = [S.sbuf("q_sb%d" % i, [64, 4, 128], BF16) for i in range(2)]
        qi_sb = [S.sbuf("qi_sb%d" % i, [64, 8, 128], BF16) for i in range(2)]
        qsq = S.sbuf("qsq", [64, 4, 128], BF16)
        tmpl = [S.sbuf("tmpl%d" % i, [128, 512], F32) for i in range(2)]
        ebuf = [S.sbuf("ebuf%d" % i, [128, 512], BF16) for i in range(4)]
        pbuf = [S.sbuf("pbuf%d" % i, [128, 512], BF16) for i in range(4)]
        pT = [S.sbuf("pT%d" % i, [128, 512], BF16) for i in range(2)]
        zer = S.sbuf("zer", [128, 260], BF16)
        y_sb = [S.sbuf("y_sb%d" % i, [128, 4, 64], F32) for i in range(2)]
        km2 = S.sbuf("km2", [128, 4], F32)
        red4 = S.sbuf("red4", [128, 4], F32)
        bmax = S.sbuf("bmax", [128, 4], F32)
        wabs = S.sbuf("wabs", [128, 8], F32)
        wsgn = S.sbuf("wsgn", [128, 8], F32)
        mm = S.sbuf("mm", [128, 4], F32)
        nm = [S.sbuf("nm%d" % i, [128, 4], F32) for i in range(2)]
        nmf = [S.sbuf("nmf%d" % i, [128, 4], F32) for i in range(2)]
        rec = S.sbuf("rec", [128, 4], F32)
        half = S.sbuf("half", [128, 32], F32)
        sm = {n: S.sbuf("bs_" + n, [128, 1], F32) for n in ["A", "lo", "nlo", "mid", "cnt", "t"]}
        S.mkbanks(4)
        plb = [S.psum("plb%d" % i, [128, 512], F32) for i in range(2)]
        pob = [S.psum("pob%d" % i, [128, 512], F32) for i in range(1)]
        ptb1 = S.psum("ptb", [128, 1024], BF16, nsub=2)

        S.op("pool", lambda e: e.memset(zer[:, :], 0.0), W=[zer])
        S.op("dve", lambda e: e.memset(km2[:, :], 0.0), W=[km2])
        S.op("dve", lambda e: e.tensor_reduce(bmax[:, :], relb[:, :, :], AX.X, ALU.max), R=[relb], W=[bmax])
        for kt in range(SEQ // 512):
            ks = slice(kt * 512, (kt + 1) * 512)
            kb = kbuf[kt % 3]
            S.dma("sp", kb[:, :, :], kThd[:, :, ks], R=[], W=[kb])
            S.op("act", lambda e: e.activation(ksq[:, :, :], kb[:, :, :], AF.Square), R=[kb], W=[ksq])
            for h in range(4):
                p = S.bank()
                S.op("pe", lambda e: e.matmul(p[:, :], ones64[:, :], ksq[:, h, :], start=True, stop=True), R=[ones64, ksq], W=[p])
                S.op("dve", lambda e: e.tensor_reduce(red4[:, h:h + 1], p[:, :], AX.X, ALU.max), R=[p], W=[red4])
            S.op("dve", lambda e: e.tensor_tensor(km2[:, :], km2[:, :], red4[:, :], ALU.max), R=[km2, red4], W=[km2])

        st = {"ki": 0, "kv": 0}

        def gen_index(j):
            nk = 1024 * (j + 1)
            nt = nk // 512
            qb = q_sb[j % 2]
            qib = qi_sb[j % 2]
            bsl = slice(j * 128, (j + 1) * 128)
            S.dma("sp", qb[:, :, :], qThd[:, :, bsl], R=[], W=[qb])
            S.dma("sp", qib[:, :, :], qiThd[:, :, bsl], R=[], W=[qib])
            S.op("act", lambda e: e.activation(wsgn[:, :], wi_sb[:, j, :], AF.Sign), R=[wi_sb], W=[wsgn])
            S.op("dve", lambda e: e.tensor_tensor(wabs[:, :], wi_sb[:, j, :], wsgn[:, :], ALU.mult), R=[wi_sb, wsgn], W=[wabs])
            S.op("act", lambda e: e.activation(qsq[:, :, :], qb[:, :, :], AF.Square), R=[qb], W=[qsq])
            pq = S.bank()
            for h in range(4):
                S.op("pe", lambda e: e.matmul(pq[:, 2 * h:2 * h + 2], qsq[:, h, :], ones64[:, 0:2], start=True, stop=True), R=[qsq, ones64], W=[pq])
            S.op("dve", lambda e: e.tensor_tensor(mm[:, :], pq[:, 0:8].rearrange("p (h two) -> p h two", two=2)[:, :, 0], km2[:, :], ALU.mult), R=[pq, km2], W=[mm])
            S.op("act", lambda e: e.activation(mm[:, :], mm[:, :], AF.Sqrt), R=[mm], W=[mm])
            S.op("dve", lambda e: e.scalar_tensor_tensor(mm[:, :], mm[:, :], 1.05 * 0.125, bmax[:, :], ALU.mult, ALU.add), R=[mm, bmax], W=[mm])
            S.op("dve", lambda e: e.tensor_scalar(nm[j % 2][:, :], mm[:, :], -1.0, None, ALU.mult), R=[mm], W=[nm[j % 2]])
            S.op("dve", lambda e: e.tensor_tensor(nmf[j % 2][:, :], b31[:, :], mm[:, :], ALU.subtract), R=[b31, mm], W=[nmf[j % 2]])
            yield
            for kt in range(nt):
                ks = slice(kt * 512, (kt + 1) * 512)
                kt_b = kib[st["ki"] % 3]
                st["ki"] += 1
                S.dma("sp", kt_b[:, :], kiTd[:, ks], R=[], W=[kt_b])
                for h in range(8):
                    p = S.bank()
                    S.op("pe", lambda e: e.matmul(p[:, :], qib[:, h, :], kt_b[:, :], start=True, stop=True), R=[qib, kt_b], W=[p])
                    S.op("act", lambda e: e.activation(p[:, :], p[:, :], AF.Relu, scale=wabs[:, h:h + 1]), R=[p, wabs], W=[p])
                    if h == 0:
                        S.op("dve", lambda e: e.tensor_scalar(score[:, ks], p[:, :], wsgn[:, 0:1], None, ALU.mult), R=[p, wsgn], W=[(score, kt)])
                    else:
                        S.op("dve", lambda e: e.scalar_tensor_tensor(score[:, ks], p[:, :], wsgn[:, h:h + 1], score[:, ks], ALU.mult, ALU.add),
                             R=[p, wsgn, (score, kt)], W=[(score, kt)])
                yield
            allr = [(score, list(range(nt)))]
            S.op("dve", lambda e: e.tensor_reduce(sm["A"][:, :], score[:, 0:nk], AX.X, ALU.max, apply_absolute_value=True), R=allr, W=[sm["A"]])
            S.op("dve", lambda e: e.tensor_tensor(score[:, nk - 1024:nk], score[:, nk - 1024:nk], pen[:, :], ALU.add),
                 R=[(score, [nt - 2, nt - 1]), pen], W=[(score, [nt - 2, nt - 1])])
            S.op("dve", lambda e: e.tensor_scalar(sm["mid"][:, :], sm["A"][:, :], 2.02, 2.0, ALU.mult, ALU.add), R=[sm["A"]], W=[sm["mid"]])
            S.op("dve", lambda e: e.tensor_scalar(sm["nlo"][:, :], sm["mid"][:, :], 0.5, None, ALU.mult), R=[sm["mid"]], W=[sm["nlo"]])
            S.op("dve", lambda e: e.tensor_scalar(half[:, :], pow2[:, :], sm["mid"][:, 0:1], None, ALU.mult), R=[pow2, sm["mid"]], W=[half])
            yield
            if j >= 2:
                g = nk // 256
                for i in range(256):
                    last = (i % 16 == 15)
                    S.op("dve", lambda e: e.max(cand[:, i * 8:(i + 1) * 8], score[:, i * g:(i + 1) * g]),
                         R=[(score, list(range((i * g) // 512, ((i + 1) * g - 1) // 512 + 1)))], W=([(cand, i // 16)] if last else []))
                    if i % 64 == 63:
                        yield
                src, nsrc, rsrc = cand, 2048, [cand]
            else:
                src, nsrc, rsrc = score, nk, allr
            thrc = float(2 * TOPK - 1 - nsrc)
            for it in range(nit):
                S.op("dve", lambda e: e.tensor_tensor(sm["mid"][:, :], sm["nlo"][:, :], half[:, it:it + 1], ALU.subtract), R=[sm["nlo"], half], W=[sm["mid"]])
                S.op("act", lambda e: e.activation(junk[:, 0:nsrc], src[:, 0:nsrc], AF.Sign, bias=sm["mid"][:, 0:1], scale=1.0, accum_out=sm["cnt"][:, 0:1]),
                     R=rsrc + [sm["mid"]], W=[sm["cnt"], junk])
                S.op("dve", lambda e: e.scalar_tensor_tensor(sm["t"][:, :], sm["cnt"][:, :], thrc, half[:, it:it + 1], ALU.is_ge, ALU.mult),
                     R=[sm["cnt"], half], W=[sm["t"]])
                S.op("dve", lambda e: e.tensor_tensor(sm["nlo"][:, :], sm["nlo"][:, :], sm["t"][:, :], ALU.subtract), R=[sm["nlo"], sm["t"]], W=[sm["nlo"]])
                if it % 2 == 1:
                    yield
            S.op("dve", lambda e: e.tensor_scalar(sm["lo"][:, :], sm["nlo"][:, :], -1.0, None, ALU.mult), R=[sm["nlo"]], W=[sm["lo"]])

        def emit_mask(j):
            nk = 1024 * (j + 1)
            for c0 in range(0, nk, 2048):
                n = min(2048, nk - c0)
                regs = [(score, list(range(c0 // 512, (c0 + n) // 512)))]
                S.op("dve", lambda e: e.tensor_scalar(maskb[:, c0:c0 + n], score[:, c0:c0 + n], sm["lo"][:, 0:1], None, ALU.is_ge, saturate=False),
                     R=regs + [sm["lo"]], W=[(maskb, list(range(c0 // 512, (c0 + n) // 512)))])

        def gen_attn(j):
            nk = 1024 * (j + 1)
            nt = nk // 512
            qb = q_sb[j % 2]
            po = pob[0]
            bsl = slice(j * 128, (j + 1) * 128)
            order = [kt for kt in (nt - 1, nt - 2, nt - 3) if kt >= 0] + list(range(0, max(nt - 3, 0)))
            S.op("pe", lambda e: e.matmul(po[:, 0:260], zer[:, 0:128], zer[:, :], start=True, stop=True), R=[zer], W=[po])
            items = [(kt, h) for kt in order for h in range(4)]
            kvb = {}

            def load_kv(kt):
                kb = kbuf[st["kv"] % 3]
                vb = vbuf[st["kv"] % 3]
                st["kv"] += 1
                ks = slice(kt * 512, (kt + 1) * 512)
                S.dma("sp", kb[:, :, :], kThd[:, :, ks], R=[], W=[kb])
                S.dma("sp", vb[:, :, :], vaugd.t.rearrange("(i p) f -> p i f", p=128)[:, kt * 4:(kt + 1) * 4, :], R=[], W=[vb])
                kvb[kt] = (kb, vb)

            def stage_a(i):
                kt, h = items[i]
                if h == 0:
                    if kt not in kvb:
                        load_kv(kt)
                    nxt = order.index(kt) + 1
                    if nxt < len(order) and order[nxt] not in kvb:
                        load_kv(order[nxt])
                kb, vb = kvb[kt]
                ks = slice(kt * 512, (kt + 1) * 512)
                pl = plb[i % 2]
                S.op("pe", lambda e: e.matmul(pl[:, :], qb[:, h, :], kb[:, h, :], start=True, stop=True), R=[qb, kb], W=[pl])
                eb = ebuf[i % 4]
                pb = pbuf[i % 4]
                if kt >= nt - 3:
                    wcol = (kt - (nt - 3)) * 512
                    tl = tmpl[i % 2]
                    S.op("dve", lambda e: e.scalar_tensor_tensor(tl[:, :], pl[:, :], 0.125, BT[:, h, wcol:wcol + 512], ALU.mult, ALU.add), R=[pl, BT], W=[tl])
                    S.op("act", lambda e: e.activation(eb[:, :], tl[:, :], AF.Exp, bias=nm[j % 2][:, h:h + 1], scale=1.0), R=[tl, nm[j % 2]], W=[eb])
                else:
                    S.op("act", lambda e: e.activation(eb[:, :], pl[:, :], AF.Exp, bias=nmf[j % 2][:, h:h + 1], scale=0.125), R=[pl, nmf[j % 2]], W=[eb])
                S.op("pool", lambda e: e.tensor_tensor(pb[:, :], eb[:, :], maskb[:, ks], ALU.mult), R=[eb, (maskb, kt)], W=[pb])

            def stage_b(i):
                kt, h = items[i]
                kb, vb = kvb[kt]
                pb = pbuf[i % 4]
                po_ = (i % 2) * 512
                pTt = pT[i % 2]
                for s_ in range(4):
                    S.op("pe", lambda e: e.transpose(ptb1[:, po_ + s_ * 128:po_ + (s_ + 1) * 128], pb[:, s_ * 128:(s_ + 1) * 128], ident[:, :]), R=[pb, ident], W=[(ptb1, i % 2)])
                if i % 2 == 0:
                    S.op("act", lambda e: e.copy(pTt[:, :], ptb1[:, po_:po_ + 512]), R=[(ptb1, i % 2)], W=[pTt])
                else:
                    S.op("dve", lambda e: e.tensor_copy(pTt[:, :], ptb1[:, po_:po_ + 512]), R=[(ptb1, i % 2)], W=[pTt])
                last = (i >= len(items) - 4)
                for s_ in range(4):
                    S.op("pe", lambda e: e.matmul(po[:, h * 65:(h + 1) * 65], pTt[:, s_ * 128:(s_ + 1) * 128], vb[:, s_, h * 65:(h + 1) * 65],
                                                  start=False, stop=False, skip_group_check=True), R=[pTt, vb], W=[po])

            n = len(items)
            for i0 in range(min(3, n)):
                stage_a(i0)
            for i in range(n):
                stage_b(i)
                if i + 3 < n:
                    stage_a(i + 3)
                if i % 2 == 1:
                    yield
            Ov = po[:, 0:260].rearrange("p (h e) -> p h e", e=65)
            S.op("dve", lambda e: e.reciprocal(rec[:, :], Ov[:, :, 64]), R=[po], W=[rec])
            ys = y_sb[j % 2]
            for h in range(4):
                S.op("dve", lambda e: e.tensor_scalar(ys[:, h, :], Ov[:, h, 0:64], rec[:, h:h + 1], None, ALU.mult), R=[po, rec], W=[ys])
            S.dma("sp", yb[bsl, :], ys[:, :, :].rearrange("p h e -> p (h e)"), R=[ys], W=[yb])

        def drain(*gens):
            gens = [g for g in gens if g is not None]
            while gens:
                for g in list(gens):
                    try:
                        next(g)
                    except StopIteration:
                        gens.remove(g)

        drain(gen_index(0))
        emit_mask(0)
        for j in range(nblk):
            drain(gen_attn(j), gen_index(j + 1) if j + 1 < nblk else None)
            if j + 1 < nblk:
                emit_mask(j + 1)
        S.finish([yb])
    return nc
```
